# Optimizing a Trainium2 kernel written in Bass

```python
import jax, jax.numpy as jnp
from jax import lax
import numpy as np

D_MODEL = 4096
BATCH = 1
SEQ = 8192
DEPTH = 1

CHUNK = 64
Q_BLOCK = 128
PLE_DIM = 256
EPS = 1e-6

POOL_WINDOWS = (2, 4, 8, 16)
POOL_GROUPS = len(POOL_WINDOWS)
POOL_WIDTH = D_MODEL // 2
POOL_GROUP_WIDTH = POOL_WIDTH // POOL_GROUPS

N_HEADS = 16
QK_NOPE_DIM = 128
QK_ROPE_DIM = 64
QK_DIM = QK_NOPE_DIM + QK_ROPE_DIM
V_HEAD_DIM = 128
MLA_WIDTH = N_HEADS * V_HEAD_DIM
Q_LORA_RANK = D_MODEL // 4
KV_LORA_RANK = 512
ROPE_THETA = 10000.0

IN_WIDTH = POOL_WIDTH + Q_LORA_RANK + KV_LORA_RANK + QK_ROPE_DIM
N_BRANCHES = 2

D_FF = -(-8 * D_MODEL // (3 * 256)) * 256

kernel_name = "hybrid_pool_mla_gated_encoder"


def rms_norm(x, g):
    xf = x.astype(jnp.float32)
    xf = xf * lax.rsqrt(jnp.mean(xf * xf, axis=-1, keepdims=True) + EPS)
    return (xf * g.astype(jnp.float32)).astype(x.dtype)


def rope_tables(positions):
    inv_freq = ROPE_THETA ** (-jnp.arange(0, QK_ROPE_DIM, 2, dtype=jnp.float32) / QK_ROPE_DIM)
    ang = positions.astype(jnp.float32)[..., None] * inv_freq
    return jnp.cos(ang), jnp.sin(ang)


def apply_rope(x, cos, sin):
    xf = x.astype(jnp.float32)
    x1, x2 = xf[..., :QK_ROPE_DIM // 2], xf[..., QK_ROPE_DIM // 2:]
    out = jnp.concatenate([x1 * cos - x2 * sin, x1 * sin + x2 * cos], axis=-1)
    return out.astype(x.dtype)


def pool_mixer(u, w_pool, pool_scale):
    b, s, _ = u.shape
    uf = u.astype(jnp.float32).reshape(b, s, POOL_GROUPS, POOL_GROUP_WIDTH)
    cs = jnp.pad(jnp.cumsum(uf, axis=1), ((0, 0), (1, 0), (0, 0), (0, 0)))
    t = jnp.arange(s)
    outs = []
    for g, w in enumerate(POOL_WINDOWS):
        start = jnp.maximum(t + 1 - w, 0)
        cnt = (t + 1 - start).astype(jnp.float32)[None, :, None]
        win_sum = cs[:, 1:, g] - cs[:, start, g]
        outs.append(win_sum / cnt - uf[:, :, g])
    pooled = jnp.stack(outs, axis=2).astype(u.dtype)
    mixed = jnp.einsum('bsgc,gcd->bsgd', pooled, w_pool)
    return mixed.reshape(b, s, POOL_WIDTH) * pool_scale


def mla_mixer(q_lat, kv_lat, k_rope, cos, sin, q_norm, kv_norm, w_q_b, w_kv_b):
    b, s, _ = q_lat.shape
    q = jnp.einsum('bsr,rhd->bshd', rms_norm(q_lat, q_norm), w_q_b)
    q_nope = q[..., :QK_NOPE_DIM]
    q_rope = apply_rope(q[..., QK_NOPE_DIM:], cos[:, :, None], sin[:, :, None])
    kv = jnp.einsum('bsr,rhd->bshd', rms_norm(kv_lat, kv_norm), w_kv_b)
    k_nope, v = kv[..., :QK_NOPE_DIM], kv[..., QK_NOPE_DIM:]
    k_rope = apply_rope(k_rope, cos, sin)
    scale = QK_DIM ** -0.5
    n_blocks = s // Q_BLOCK

    def to_blocks(a):
        return a.reshape(b, n_blocks, Q_BLOCK, *a.shape[2:]).swapaxes(0, 1)

    key_chunk = jnp.arange(s) // CHUNK
    q_starts = jnp.arange(n_blocks) * Q_BLOCK

    def attend(args):
        qn, qr, start = args
        sc = (jnp.einsum('bqhd,bkhd->bhqk', qn, k_nope, preferred_element_type=jnp.float32)
              + jnp.einsum('bqhd,bkd->bhqk', qr, k_rope, preferred_element_type=jnp.float32)) * scale
        q_chunk = (start + jnp.arange(Q_BLOCK)) // CHUNK
        mask = key_chunk[None, :] <= q_chunk[:, None]
        sc = jnp.where(mask[None, None], sc, -jnp.inf)
        pr = jax.nn.softmax(sc, axis=-1).astype(v.dtype)
        return jnp.einsum('bhqk,bkhd->bqhd', pr, v)

    out = lax.map(attend, (to_blocks(q_nope), to_blocks(q_rope), q_starts))
    return out.swapaxes(0, 1).reshape(b, s, MLA_WIDTH)


def setup_inputs(seed: int = 0) -> dict:
    key = jax.random.key(seed)
    ks = jax.random.split(key, 32)
    f32 = jnp.float32

    def dense(k, shape, fan_in):
        return jax.random.normal(k, shape, f32) * (fan_in ** -0.5)

    def gain(k, dim):
        return 1.0 + 0.1 * jax.random.normal(k, (DEPTH, dim), f32)

    offset = jax.random.randint(ks[2], (BATCH, 1), 0, 1024, dtype=jnp.int32)
    positions = offset + jnp.arange(SEQ, dtype=jnp.int32)[None, :]
    return {
        "x": jax.random.normal(ks[0], (BATCH, SEQ, D_MODEL), f32),
        "p": jax.random.normal(ks[1], (DEPTH, BATCH, SEQ, PLE_DIM), f32),
        "positions": positions,
        "norm_mix_pre": gain(ks[3], D_MODEL),
        "norm_mix_post": gain(ks[4], D_MODEL),
        "w_in": dense(ks[5], (DEPTH, D_MODEL, IN_WIDTH), D_MODEL),
        "q_norm": gain(ks[6], Q_LORA_RANK),
        "kv_norm": gain(ks[7], KV_LORA_RANK),
        "w_q_b": dense(ks[8], (DEPTH, Q_LORA_RANK, N_HEADS, QK_DIM), Q_LORA_RANK),
        "w_kv_b": dense(ks[9], (DEPTH, KV_LORA_RANK, N_HEADS, QK_NOPE_DIM + V_HEAD_DIM), KV_LORA_RANK),
        "w_pool": dense(ks[10], (DEPTH, POOL_GROUPS, POOL_GROUP_WIDTH, POOL_GROUP_WIDTH), POOL_GROUP_WIDTH),
        "pool_scale": gain(ks[11], POOL_WIDTH),
        "w_up_pool": dense(ks[12], (DEPTH, POOL_WIDTH, D_MODEL), POOL_WIDTH),
        "w_up_mla": dense(ks[13], (DEPTH, MLA_WIDTH, D_MODEL), MLA_WIDTH),
        "w_branch_gate": dense(ks[14], (DEPTH, D_MODEL, N_BRANCHES, D_MODEL), D_MODEL),
        "w_out": dense(ks[15], (DEPTH, D_MODEL, D_MODEL), D_MODEL),
        "norm_ffn_pre": gain(ks[16], D_MODEL),
        "norm_ffn_post": gain(ks[17], D_MODEL),
        "w_ffn_gate": dense(ks[18], (DEPTH, D_MODEL, D_FF), D_MODEL),
        "w_ffn_up": dense(ks[19], (DEPTH, D_MODEL, D_FF), D_MODEL),
        "w_ffn_down": dense(ks[20], (DEPTH, D_FF, D_MODEL), D_FF),
        "norm_ple_pre": gain(ks[21], D_MODEL),
        "w_ple_gate": dense(ks[22], (DEPTH, D_MODEL, D_MODEL), D_MODEL),
        "w_ple_proj": dense(ks[23], (DEPTH, PLE_DIM, D_MODEL), PLE_DIM),
        "norm_ple_post": gain(ks[24], D_MODEL),
    }


def reference(x, p, positions, norm_mix_pre, norm_mix_post, w_in, q_norm, kv_norm, w_q_b, w_kv_b,
              w_pool, pool_scale, w_up_pool, w_up_mla, w_branch_gate, w_out, norm_ffn_pre,
              norm_ffn_post, w_ffn_gate, w_ffn_up, w_ffn_down, norm_ple_pre, w_ple_gate,
              w_ple_proj, norm_ple_post):
    cos, sin = rope_tables(positions)
    o1 = POOL_WIDTH
    o2 = o1 + Q_LORA_RANK
    o3 = o2 + KV_LORA_RANK
    for i in range(DEPTH):
        h = rms_norm(x, norm_mix_pre[i])
        z = jnp.einsum('bsd,de->bse', h, w_in[i])
        u_pool, q_lat, kv_lat, k_rope = z[..., :o1], z[..., o1:o2], z[..., o2:o3], z[..., o3:]
        ya = jnp.einsum('bsc,cd->bsd', pool_mixer(u_pool, w_pool[i], pool_scale[i]), w_up_pool[i])
        yb = jnp.einsum('bsc,cd->bsd',
                        mla_mixer(q_lat, kv_lat, k_rope, cos, sin, q_norm[i], kv_norm[i],
                                  w_q_b[i], w_kv_b[i]),
                        w_up_mla[i])
        gates = jax.nn.sigmoid(jnp.einsum('bsd,dge->bsge', h, w_branch_gate[i]))
        merged = gates[:, :, 0] * ya + gates[:, :, 1] * yb
        mix = jnp.einsum('bsd,de->bse', merged, w_out[i])
        x = x + rms_norm(mix, norm_mix_post[i])
        h2 = rms_norm(x, norm_ffn_pre[i])
        act = jax.nn.silu(jnp.einsum('bsd,df->bsf', h2, w_ffn_gate[i])) * jnp.einsum('bsd,df->bsf', h2, w_ffn_up[i])
        ffn = jnp.einsum('bsf,fd->bsd', act, w_ffn_down[i])
        x = x + rms_norm(ffn, norm_ffn_post[i])
        gate = jax.nn.sigmoid(jnp.einsum('bsd,de->bse', rms_norm(x, norm_ple_pre[i]), w_ple_gate[i]))
        pe = jnp.einsum('bsr,rd->bsd', p[i].astype(x.dtype), w_ple_proj[i])
        x = x + rms_norm(pe * gate, norm_ple_post[i])
    return x
```

```python
import numpy as np
import concourse.bass as bass
import concourse.mybir as mybir
from concourse.bass_utils import run_bass_kernel_spmd
from contextlib import ExitStack

F32, BF16, I32 = mybir.dt.float32, mybir.dt.bfloat16, mybir.dt.int32
AF = mybir.ActivationFunctionType
OP = mybir.AluOpType

NCORE = 8
S = 8192
D = 4096
NT = 1024
NBLK = 8
FT = 32
DFF = 11008
FFT = 86
NH = 16
EPS = 1e-6
SCALE = 192 ** -0.5
NEG = -30000.0
PI = float(np.pi)

C_GMIXPRE, C_GMIXPOST, C_GFFNPRE, C_GFFNPOST, C_GPLEPRE, C_GPLEPOST = 0, 32, 64, 96, 128, 160
C_QN, C_KVN, C_PSC, C_INVF, C_EPS, C_MASK, C_PCORR, C_NCOL = 192, 200, 204, 220, 221, 222, 238, 304


class Tok:
    __slots__ = ("w", "rs", "excl")

    def __init__(self):
        self.w = None
        self.rs = []
        self.excl = False


class Op:
    __slots__ = ("eng", "fn", "deps", "signal", "count", "sem", "val", "is_dma")


class Prog:
    ENG = ("pe", "act", "dve", "pool", "sp")

    def __init__(self):
        self.streams = {e: [] for e in self.ENG}
        self.dma_val = {}
        self.all_toks = []
        self.last_barrier = None

    def tok(self):
        t = Tok()
        t.w = self.last_barrier
        self.all_toks.append(t)
        return t

    def toks(self, n):
        return [self.tok() for _ in range(n)]

    def op(self, eng, fn, reads=(), writes=(), dma_sem=None):
        o = Op()
        o.eng = eng
        o.fn = fn
        o.signal = False
        o.count = 0
        o.is_dma = dma_sem is not None
        o.sem = dma_sem
        o.val = 0
        if o.is_dma:
            k = id(dma_sem)
            self.dma_val[k] = self.dma_val.get(k, 0) + 16
            o.val = self.dma_val[k]
        if any(t.excl for t in reads):
            writes = list(writes) + [t for t in reads if t.excl]
            reads = [t for t in reads if not t.excl]
        deps = []
        seen = set()

        def add(d):
            if d is None or id(d) in seen:
                return
            seen.add(id(d))
            if (not d.is_dma) and (not o.is_dma) and d.eng == eng and eng == "pe":
                return
            deps.append(d)
            if not d.is_dma:
                d.signal = True

        for t in reads:
            add(t.w)
        for t in writes:
            add(t.w)
            for r in t.rs:
                add(r)
        o.deps = deps
        for t in reads:
            if not o.is_dma:
                t.rs = [r for r in t.rs if r.is_dma or r.eng != eng]
            t.rs.append(o)
        for t in writes:
            t.w = o
            t.rs = []
        self.streams[eng].append(o)
        return o

    def barrier(self, dummy_ap, skip=()):
        sk = set(id(t) for t in skip)
        self.last_barrier = self.op("dve", lambda e: e.memset(dummy_ap, 0.0), writes=[t for t in self.all_toks if id(t) not in sk])

    def finalize(self, nc, es):
        esem = {e: es.enter_context(nc.semaphore("es_" + e)) for e in ("pe", "act", "dve", "pool")}
        for e in self.ENG:
            cnt = 0
            for o in self.streams[e]:
                if (not o.is_dma) and o.signal:
                    cnt += 1
                    o.count = cnt
        block = es.enter_context(nc.Block())
        streams = self.streams

        def run(ename):
            def body(eng):
                clock = {}
                for o in streams[ename]:
                    for d in o.deps:
                        if d.is_dma:
                            key = ("d", id(d.sem))
                            sem = d.sem
                            val = d.val
                        else:
                            key = d.eng
                            sem = esem[d.eng]
                            val = d.count
                        if clock.get(key, 0) < val:
                            eng.wait_ge(sem, val)
                            clock[key] = val
                    ins = o.fn(eng)
                    if o.is_dma:
                        ins.then_inc(o.sem, 16)
                    elif o.signal:
                        ins.then_inc(esem[ename], 1)
            return body

        block.tensor(run("pe"))
        block.scalar(run("act"))
        block.vector(run("dve"))
        block.gpsimd(run("pool"))
        block.sync(run("sp"))


class Arena:
    def __init__(self, t, nbytes):
        self.t = t
        self.n = nbytes
        self.off = 0

    def reset(self, off=0):
        self.off = off

    def alloc(self, shape, dtype, parts=128):
        esz = 4 if dtype in (F32, I32) else 2
        n = 1
        for s in shape:
            n *= s
        nb = (n * esz + 63) // 64 * 64
        assert self.off + nb <= self.n, ("arena overflow", self.off, nb, self.n)
        a = self.t[0:parts, self.off // 2:(self.off + n * esz) // 2]
        self.off += nb
        if dtype != BF16:
            a = a.bitcast(dtype)
        if len(shape) == 2:
            a = a.rearrange("p (a b) -> p a b", b=shape[1])
        elif len(shape) == 3:
            a = a.rearrange("p (a b c) -> p a b c", b=shape[1], c=shape[2])
        return a


def build_program(stop_after=None, debug=False):
    nc = bass.Bass("TRN2", target_bir_lowering=False)
    P = Prog()

    def din(name, shape, dt=F32):
        return nc.dram_tensor(name, list(shape), dt, kind="ExternalInput").ap()

    def dscr(name, shape, dt):
        return nc.dram_tensor(name, list(shape), dt, kind=("ExternalOutput" if debug else "Internal")).ap()

    x_all = din("x_all", [S, D])
    x_own = din("x_own", [NT, D])
    x_halo = din("x_halo", [128, D])
    p_own = din("p_own", [NT, 256])
    pos_all = din("pos_all", [64, S], I32)
    pos_own = din("pos_own", [64, NT], I32)
    cst_d = din("cst", [128, C_NCOL])
    mats_d = din("mats", [128, 3, 128])
    maskm_d = din("maskm", [128, 8, 128])
    w_in_main = din("w_in_main", [24, 128, 32, 128])
    w_in_kv = din("w_in_kv", [128, 32, 576])
    w_k = din("w_k", [128, 4, NH * 128])
    w_v = din("w_v", [128, 4, NH * 128])
    w_q = din("w_q", [NH, 128, 8, 192])
    w_pool = din("w_pool", [16, 128, 4, 128])
    w_up_pool = din("w_up_pool", [32, 128, 16, 128])
    w_up_mla = din("w_up_mla", [32, 128, 16, 128])
    w_gate = din("w_gate", [64, 128, 32, 128])
    w_out = din("w_out", [32, 128, 32, 128])
    w_fg = din("w_fg", [FFT, 128, 32, 128])
    w_fu = din("w_fu", [FFT, 128, 32, 128])
    w_fd = din("w_fd", [32, 128, FFT, 128])
    w_pg = din("w_pg", [32, 128, 32, 128])
    w_pp = din("w_pp", [32, 128, 2, 128])
    out_d = nc.dram_tensor("out", [NT, D], F32, kind="ExternalOutput").ap()

    K_scr = dscr("K_scr", [NH, 128, S], BF16)
    V_scr = dscr("V_scr", [NH, 64, 128, 128], BF16)
    KVN_scr = dscr("KVN_scr", [4, 128, S], BF16)
    T_KVN = P.toks(16)
    GA_scr = dscr("GA_scr", [32, 128, NT], BF16)
    GB_scr = dscr("GB_scr", [32, 128, NT], BF16)
    PM_scr = dscr("PM_scr", [16, 128, NT], BF16)
    MIX_scr = dscr("MIX_scr", [32, 128, NT], F32)
    X1_scr = dscr("X1_scr", [32, 128, NT], F32)
    X2_scr = dscr("X2_scr", [32, 128, NT], F32)
    ACT_scr = dscr("ACT_scr", [FFT, 128, NT], BF16)
    FFN_scr = dscr("FFN_scr", [32, 128, NT], F32)
    PRD_scr = dscr("PRD_scr", [32, 128, NT], F32)
    T_K = [[P.tok() for _ in range(32)] for _ in range(4)]
    T_V = [P.tok() for _ in range(64)]
    T_GA, T_GB = P.toks(32), P.toks(32)
    T_PM = P.toks(16)
    T_MIX, T_X1, T_X2, T_FFN, T_PRD = P.toks(32), P.toks(32), P.toks(32), [P.toks(2) for _ in range(32)], P.toks(32)
    T_ACT = P.toks(FFT)
    T_OUT = P.toks(32)

    with ExitStack() as es:
        ARENA_BYTES = 148 * 1024
        NW = 6
        FULL_BYTES = ARENA_BYTES + NW * 8192
        arena_t = es.enter_context(nc.sbuf_tensor("arena", [128, FULL_BYTES // 2], BF16))
        A = Arena(arena_t, ARENA_BYTES)
        cst = es.enter_context(nc.sbuf_tensor("cst_sb", [128, C_NCOL], F32))
        mats_bf = es.enter_context(nc.sbuf_tensor("mats_bf", [128, 3, 128], BF16))
        mats_f = es.enter_context(nc.sbuf_tensor("mats_f", [128, 2, 128], F32))
        dummy = es.enter_context(nc.sbuf_tensor("dummy_sb", [128, 8], F32))
        wring = [arena_t[:, ARENA_BYTES // 2 + i * 4096:ARENA_BYTES // 2 + (i + 1) * 4096] for i in range(NW)]
        wtok = P.toks(NW)
        wsem = [es.enter_context(nc.semaphore(f"wsem{i}")) for i in range(NW)]
        wstate = {"i": 0}
        psum = [es.enter_context(nc.psum_tensor(f"ps{i}", [128, 512], F32)) for i in range(8)]
        ptok = P.toks(8)
        for t_ in ptok:
            t_.excl = True
        pstate = {"i": 0}
        NSEM = 84
        gsem = [es.enter_context(nc.semaphore(f"gsem{i}")) for i in range(NSEM)]
        gstate = {"i": 0}
        t_cst, t_mats = P.tok(), P.tok()
        rope_sem = es.enter_context(nc.semaphore("rope_sem"))

        ident_bf = mats_bf[:, 0, :]
        ones_bf = mats_bf[:, 1, :]
        rot_bf = mats_bf[:, 2, :]
        ident_f = mats_f[:, 0, :]

        def sem():
            s_ = gsem[gstate["i"] % NSEM]
            gstate["i"] += 1
            return s_

        def ps_next(nrr=6):
            k = pstate["i"] % nrr
            pstate["i"] += 1
            return psum[k], ptok[k]

        def wslot():
            k = wstate["i"] % NW
            wstate["i"] += 1
            return wring[k], wtok[k], wsem[k]

        class SRing:
            def __init__(self, n, shape, dtype, parts=128, with_sem=True):
                self.aps = [A.alloc(shape, dtype, parts) for _ in range(n)]
                self.tk = P.toks(n)
                self.sm = [sem() for _ in range(n)] if with_sem else [None] * n
                self.i = 0
                self.n = n

            def next(self):
                k = self.i % self.n
                self.i += 1
                return self.aps[k], self.tk[k], self.sm[k]

        cc = lambda c: cst[:, c:c + 1]

        P.op("sp", lambda e: e.dma_start(out=cst[:], in_=cst_d), writes=[t_cst], dma_sem=sem())
        P.op("pool", lambda e: e.dma_start(out=mats_bf[:], in_=mats_d), writes=[t_mats], dma_sem=sem())
        P.op("sp", lambda e: e.dma_start(out=mats_f[:], in_=mats_d[:, 0:2, :]), writes=[t_mats], dma_sem=sem())

        evq = {"i": 0}

        def evac_eng():
            evq["i"] += 1
            return "act" if evq["i"] % 2 else "dve"

        def scale_copy(eng, out, in_, scale_ap, reads, writes):
            if eng == "act":
                P.op("act", lambda e: e.activation(out=out, in_=in_, func=AF.Copy, scale=scale_ap), reads=reads, writes=writes)
            else:
                P.op("dve", lambda e: e.tensor_scalar(out=out, in0=in_, scalar1=scale_ap, scalar2=None, op0=OP.mult), reads=reads, writes=writes)

        def plain_copy(eng, out, in_, reads, writes):
            if eng == "act":
                P.op("act", lambda e: e.activation(out=out, in_=in_, func=AF.Copy), reads=reads, writes=writes)
            else:
                P.op("dve", lambda e: e.tensor_copy(out=out, in_=in_), reads=reads, writes=writes)

        def rstd_from_psum(ps_list, n_feat, rstd_ap, rstd_tok, chunks):
            for (pa, pt), (cs, cn) in zip(ps_list, chunks):
                P.op("act", lambda e, pa=pa, cs=cs, cn=cn: e.activation(out=rstd_ap[:, cs:cs + cn], in_=pa[:, 0:cn], func=AF.Sqrt, bias=cc(C_EPS), scale=1.0 / n_feat),
                     reads=[pt, t_cst], writes=[rstd_tok])
            P.op("dve", lambda e: e.reciprocal(out=rstd_ap, in_=rstd_ap), reads=[rstd_tok], writes=[rstd_tok])

        def rope_tables(pos_ap, n, cos_ap, sin_ap, t_out, tmp):
            pi_t, pf, kf, ki, ang, t_tmp = tmp
            P.op("sp", lambda e: e.dma_start(out=pi_t, in_=pos_ap), writes=[t_tmp], dma_sem=rope_sem)
            P.op("dve", lambda e: e.tensor_copy(out=pf, in_=pi_t), reads=[t_tmp], writes=[t_tmp])
            P.op("dve", lambda e: e.tensor_scalar(out=pf, in0=pf, scalar1=cst[0:64, C_INVF:C_INVF + 1], scalar2=None, op0=OP.mult), reads=[t_tmp, t_cst], writes=[t_tmp])
            for dst, shift in ((sin_ap, 0.0), (cos_ap, PI / 2)):
                P.op("dve", lambda e, shift=shift: e.tensor_scalar(out=ki, in0=pf, scalar1=shift, scalar2=float(1 / (2 * PI)), op0=OP.add, op1=OP.mult), reads=[t_tmp], writes=[t_tmp])
                P.op("dve", lambda e: e.tensor_copy(out=kf, in_=ki), reads=[t_tmp], writes=[t_tmp])
                P.op("dve", lambda e: e.scalar_tensor_tensor(out=ang, in0=kf, scalar=float(-2 * PI), in1=pf, op0=OP.mult, op1=OP.add), reads=[t_tmp], writes=[t_tmp])
                if shift != 0.0:
                    P.op("dve", lambda e, shift=shift: e.tensor_scalar(out=ang, in0=ang, scalar1=shift, scalar2=None, op0=OP.add), reads=[t_tmp], writes=[t_tmp])
                P.op("dve", lambda e: e.tensor_scalar(out=kf, in0=ang, scalar1=PI, scalar2=float(-2 * PI), op0=OP.is_gt, op1=OP.mult), reads=[t_tmp], writes=[t_tmp])
                P.op("dve", lambda e: e.tensor_tensor(out=ang, in0=ang, in1=kf, op=OP.add), reads=[t_tmp], writes=[t_tmp])
                P.op("dve", lambda e: e.tensor_scalar(out=kf, in0=ang, scalar1=-PI, scalar2=float(2 * PI), op0=OP.is_lt, op1=OP.mult), reads=[t_tmp], writes=[t_tmp])
                P.op("dve", lambda e: e.tensor_tensor(out=ang, in0=ang, in1=kf, op=OP.add), reads=[t_tmp], writes=[t_tmp])
                P.op("act", lambda e, dst=dst: e.activation(out=dst, in_=ang, func=AF.Sin), reads=[t_tmp], writes=[t_out])

        def rope_tmp(n):
            return (A.alloc([n], I32, 64), A.alloc([n], F32, 64), A.alloc([n], F32, 64), A.alloc([n], I32, 64), A.alloc([n], F32, 64), P.tok())

        def front_end(rows_fn, nblk, xring, junk, t_junk, ssb, t_ss, gain_c0, hT, hT_tok, col0, KT=FT, width=D):
            xbs = fe_load(rows_fn, nblk, xring, junk, t_junk, ssb, t_ss, gain_c0, width)
            fe_transpose(xbs, nblk, gain_c0, hT, hT_tok, col0, KT)

        def fe_load(rows_fn, nblk, xring, junk, t_junk, ssb, t_ss, gain_c0, width=D):
            xbs = []
            for i in range(nblk):
                xb, t_xb, s_xb = xring.next()
                P.op("pool", lambda e, xb=xb, i=i: e.dma_start(out=xb, in_=rows_fn(i)), writes=[t_xb], dma_sem=s_xb)
                if gain_c0 is not None:
                    P.op("act", lambda e, xb=xb, i=i: e.activation(out=junk, in_=xb, func=AF.Square, accum_out=ssb[:, i:i + 1]), reads=[t_xb], writes=[t_junk, t_ss[i]])
                    P.op("act", lambda e, i=i: e.activation(out=ssb[:, i:i + 1], in_=ssb[:, i:i + 1], func=AF.Sqrt, bias=cc(C_EPS), scale=1.0 / width), reads=[t_ss[i], t_cst], writes=[t_ss[i]])
                    P.op("dve", lambda e, i=i: e.reciprocal(out=ssb[:, i:i + 1], in_=ssb[:, i:i + 1]), reads=[t_ss[i]], writes=[t_ss[i]])
                    P.op("dve", lambda e, xb=xb, i=i: e.tensor_scalar(out=xb, in0=xb, scalar1=ssb[:, i:i + 1], scalar2=None, op0=OP.mult), reads=[t_ss[i], t_xb], writes=[t_xb])
                xbs.append((xb, t_xb))
            return xbs

        def fe_transpose(xbs, nblk, gain_c0, hT, hT_tok, col0, KT=FT):
            for ft in range(KT):
                pa, pt = ps_next()
                pab = pa[:].bitcast(BF16)
                for i, (xb, t_xb) in enumerate(xbs):
                    P.op("pe", lambda e, pab=pab, xb=xb, i=i, ft=ft: e.transpose(out=pab[:, i * 128:(i + 1) * 128], in_=xb[:, ft * 128:(ft + 1) * 128], identity=ident_bf),
                         reads=[t_xb, t_mats], writes=[pt])
                eng = evac_eng()
                o_ap = hT[:, ft, col0:col0 + nblk * 128]
                i_ap = pab[:, 0:nblk * 128]
                if gain_c0 is not None:
                    scale_copy(eng, o_ap, i_ap, cc(gain_c0 + ft), [pt, t_cst], [hT_tok[ft]])
                else:
                    plain_copy(eng, o_ap, i_ap, [pt], [hT_tok[ft]])

        def load_w(src_ap, kc, n=128):
            wt, t_w, s_w = wslot()
            wv_ = wt[:, 0:kc * n].rearrange("p (k n) -> p k n", n=n)
            P.op("pool", lambda e: e.dma_start(out=wv_, in_=src_ap), writes=[t_w], dma_sem=s_w)
            return wv_, t_w

        def proj(Wt, m, KT, chunks, nrr=6, kchunk=32, mcols=128):
            pss = [ps_next(nrr) for _ in chunks]
            k0 = 0
            while k0 < KT:
                kc = min(kchunk, KT - k0)
                wv_, t_w = load_w(Wt[m, :, k0:k0 + kc, :], kc)
                for kk in range(kc):
                    kt = k0 + kk
                    for (pa, pt), (rfn, cs, cn) in zip(pss, chunks):
                        ra, rt = rfn(kt)
                        P.op("pe", lambda e, pa=pa, wv_=wv_, kk=kk, ra=ra, cs=cs, cn=cn, kt=kt: e.matmul(pa[0:mcols, 0:cn], lhsT=wv_[:, kk, 0:mcols], rhs=ra[:, cs:cs + cn], start=(kt == 0), stop=(kt == KT - 1)),
                             reads=[t_w, rt], writes=[pt])
                k0 += kc
            return pss

        CH2 = [(0, 512), (512, 512)]

        def sumsq_accum(sq_ap, t_sq, acc, first, last, cn):
            pa, pt = acc
            P.op("pe", lambda e: e.matmul(pa[:, 0:cn], lhsT=ones_bf, rhs=sq_ap, start=first, stop=last), reads=[t_sq, t_mats], writes=[pt])

        class _Stop(Exception):
            pass

        def ckpt(name):
            if stop_after == name:
                raise _Stop()

        def body():
            A.n = FULL_BYTES
            A.reset(0)
            kr = A.alloc([S], BF16, 64)
            GA_ = 4
            NG = S // (GA_ * 128)
            GT = GA_ * 128
            t_kr = P.toks(NG)
            A_BASE1 = A.off
            wkvin = A.alloc([32, 576], BF16)
            t_wres = P.tok()
            P.op("pool", lambda e: e.dma_start(out=wkvin, in_=w_in_kv), writes=[t_wres], dma_sem=sem())
            xring = SRing(7, [D], BF16)
            junk = A.alloc([D], BF16)
            t_junk = P.tok()
            ssb = A.alloc([8], F32)
            t_ss = P.toks(8)
            hTa = A.alloc([FT, GT], BF16)
            t_hTa = P.toks(FT)
            kvl = A.alloc([4, GT], F32)
            t_kvl = P.toks(4)
            sqr = SRing(2, [GT], BF16, with_sem=False)
            rstd_a = A.alloc([GT], F32)
            t_rstd_a = P.tok()
            kvnr = SRing(2, [4, GT], BF16)
            zr = A.alloc([GT], BF16, 64)
            t_zr = P.tok()
            cos_a = A.alloc([GT], F32, 64)
            sin_a = A.alloc([GT], F32, 64)
            t_cs_a = P.tok()
            rtmp = rope_tmp(GT)
            rt1 = A.alloc([GT], F32, 64)
            rt2 = A.alloc([GT], F32, 64)
            t_rt = P.tok()

            xbs_cur = fe_load(lambda i: x_all[i * 128:(i + 1) * 128, :], GA_, xring, junk, t_junk, ssb, t_ss, C_GMIXPRE)
            for g in range(NG):
                fe_transpose(xbs_cur, GA_, C_GMIXPRE, hTa, t_hTa, 0)
                rope_tables(pos_all[:, g * GT:(g + 1) * GT], GT, cos_a, sin_a, t_cs_a, rtmp)
                if g + 1 < NG:
                    xbs_cur = fe_load(lambda i, g=g: x_all[((g + 1) * GA_ + i) * 128:((g + 1) * GA_ + i + 1) * 128, :], GA_, xring, junk, t_junk, ssb, t_ss, C_GMIXPRE)
                pkv = []
                for m in range(4):
                    pa, pt = ps_next()
                    for kt in range(FT):
                        P.op("pe", lambda e, pa=pa, kt=kt, m=m: e.matmul(pa[:, 0:GT], lhsT=wkvin[:, kt, m * 128:(m + 1) * 128], rhs=hTa[:, kt, :], start=(kt == 0), stop=(kt == FT - 1)),
                             reads=[t_wres, t_hTa[kt]], writes=[pt])
                    pkv.append((pa, pt))
                pr, ptr = ps_next()
                for kt in range(FT):
                    P.op("pe", lambda e, kt=kt, pr=pr: e.matmul(pr[0:64, 0:GT], lhsT=wkvin[:, kt, 512:576], rhs=hTa[:, kt, :], start=(kt == 0), stop=(kt == FT - 1)),
                         reads=[t_wres, t_hTa[kt]], writes=[ptr])
                acc = (psum[7], ptok[7])
                for m in range(4):
                    pa, pt = pkv[m]
                    P.op("act", lambda e, pa=pa, m=m: e.activation(out=kvl[:, m, :], in_=pa[:, 0:GT], func=AF.Copy), reads=[pt], writes=[t_kvl[m]])
                    sq, t_sq, _ = sqr.next()
                    P.op("act", lambda e, pa=pa, sq=sq: e.activation(out=sq, in_=pa[:, 0:GT], func=AF.Square), reads=[pt], writes=[t_sq])
                    sumsq_accum(sq, t_sq, acc, m == 0, m == 3, GT)
                rstd_from_psum([acc], 512, rstd_a, t_rstd_a, [(0, GT)])
                kvn, t_kvn, s_kvn = kvnr.next()
                for m in range(4):
                    P.op("dve", lambda e, m=m, kvn=kvn: e.scalar_tensor_tensor(out=kvn[:, m, :], in0=kvl[:, m, :], scalar=cc(C_KVN + m), in1=rstd_a, op0=OP.mult, op1=OP.mult),
                         reads=[t_kvl[m], t_rstd_a, t_cst], writes=[t_kvn])
                P.op("sp", lambda e, kvn=kvn, g=g: e.dma_start(out=KVN_scr[:, :, g * GT:(g + 1) * GT].rearrange("m p t -> p m t"), in_=kvn),
                     reads=[t_kvn], writes=[T_KVN[g]], dma_sem=s_kvn)
                P.op("act", lambda e, pr=pr: e.activation(out=zr, in_=pr[0:64, 0:GT], func=AF.Copy), reads=[ptr], writes=[t_zr])
                prot, ptrot = ps_next()
                P.op("pe", lambda e, prot=prot: e.matmul(prot[0:64, 0:GT], lhsT=rot_bf[0:64, 0:64], rhs=zr, start=True, stop=True), reads=[t_zr, t_mats], writes=[ptrot])
                P.op("dve", lambda e: e.tensor_tensor(out=rt1, in0=zr, in1=cos_a, op=OP.mult), reads=[t_zr, t_cs_a], writes=[t_rt])
                P.op("dve", lambda e, prot=prot: e.tensor_tensor(out=rt2, in0=prot[0:64, 0:GT], in1=sin_a, op=OP.mult), reads=[ptrot, t_cs_a], writes=[t_rt])
                P.op("dve", lambda e, g=g: e.tensor_tensor(out=kr[:, g * GT:(g + 1) * GT], in0=rt1, in1=rt2, op=OP.add), reads=[t_rt], writes=[t_kr[g]])
            P.barrier(dummy[:, 0:1])

            A.reset(A_BASE1)
            wk_sb = A.alloc([4, NH * 128], BF16)
            wv_sb = A.alloc([4, NH * 128], BF16)
            t_wres2 = P.tok()
            P.op("pool", lambda e: e.dma_start(out=wk_sb, in_=w_k), writes=[t_wres2], dma_sem=sem())
            P.op("pool", lambda e: e.dma_start(out=wv_sb, in_=w_v), writes=[t_wres2], dma_sem=sem())
            kvin = SRing(3, [4, GT], BF16)
            kst = SRing(2, [4, GT], BF16)
            vst = SRing(2, [NH, 128], BF16)
            for g in range(NG):
                kvn, t_kvn, s_kvn = kvin.next()
                P.op("sp", lambda e, kvn=kvn, g=g: e.dma_start(out=kvn, in_=KVN_scr[:, :, g * GT:(g + 1) * GT].rearrange("m p t -> p m t")),
                     reads=[T_KVN[g]], writes=[t_kvn], dma_sem=s_kvn)
                for hg in range(4):
                    ks, t_ks, s_ks = kst.next()
                    for hh in range(4):
                        h = hg * 4 + hh
                        pa, pt = ps_next(8)
                        for kt in range(4):
                            P.op("pe", lambda e, pa=pa, kt=kt, h=h, kvn=kvn: e.matmul(pa[:, 0:GT], lhsT=wk_sb[:, kt, h * 128:(h + 1) * 128], rhs=kvn[:, kt, :], start=(kt == 0), stop=(kt == 3)),
                                 reads=[t_wres2, t_kvn], writes=[pt])
                        plain_copy(evac_eng(), ks[:, hh, :], pa[:, 0:GT], [pt], [t_ks])
                    P.op("sp", lambda e, ks=ks, hg=hg, g=g: e.dma_start(out=K_scr[hg * 4:(hg + 1) * 4, :, g * GT:(g + 1) * GT].rearrange("h p t -> p h t"), in_=ks),
                         reads=[t_ks], writes=[T_K[hg][g]], dma_sem=s_ks)
                for j in range(GA_):
                    vs, t_vs, s_vs = vst.next()
                    for hg in range(4):
                        pa, pt = ps_next(8)
                        for kt in range(4):
                            P.op("pe", lambda e, pa=pa, kt=kt, j=j, hg=hg, kvn=kvn: e.matmul(pa[:, 0:512], lhsT=kvn[:, kt, j * 128:(j + 1) * 128], rhs=wv_sb[:, kt, hg * 512:(hg + 1) * 512], start=(kt == 0), stop=(kt == 3)),
                                 reads=[t_wres2, t_kvn], writes=[pt])
                        plain_copy(evac_eng(), vs[:, hg * 4:(hg + 1) * 4, :], pa[:, 0:512].rearrange("p (h d) -> p h d", d=128), [pt], [t_vs])
                    ktg = g * GA_ + j
                    P.op("sp", lambda e, vs=vs, ktg=ktg: e.dma_start(out=V_scr[:, ktg, :, :].rearrange("h p d -> p h d"), in_=vs),
                         reads=[t_vs], writes=[T_V[ktg]], dma_sem=s_vs)
            A.n = ARENA_BYTES

            P.barrier(dummy[:, 0:1])
            ckpt(0)

            A.reset(A_BASE1)
            qn = A.alloc([8, NT], BF16)
            t_qn = P.toks(8)
            A_BASE2 = A.off
            hT = A.alloc([FT, NT], BF16)
            t_hT = P.toks(FT)
            hTh = A.alloc([FT, 128], BF16)
            t_hTh = P.toks(FT)
            B_MARK = A.off
            xring = SRing(4, [D], BF16)
            junk = A.alloc([D], BF16)
            t_junk = P.tok()
            ssb = A.alloc([8], F32)
            t_ss = P.toks(8)
            for gg in range(2):
                front_end(lambda i, gg=gg: x_own[(gg * 4 + i) * 128:(gg * 4 + i + 1) * 128, :], 4, xring, junk, t_junk, ssb, t_ss, C_GMIXPRE, hT, t_hT, gg * 512)
            front_end(lambda i: x_halo, 1, xring, junk, t_junk, ssb, t_ss, C_GMIXPRE, hTh, t_hTh, 0)
            P.barrier(dummy[:, 0:1], skip=wtok)
            ckpt(1)
            A.reset(B_MARK)
            rh = lambda kt: (hT[:, kt, :], t_hT[kt])
            rhh = lambda kt: (hTh[:, kt, :], t_hTh[kt])
            CH_OWN = [(rh, 0, 512), (rh, 512, 512)]

            sgst = SRing(3, [NT], BF16)
            for m in range(64):
                pss = proj(w_gate, m, FT, CH_OWN)
                st, t_st, s_st = sgst.next()
                for (pa, pt), (cs, cn) in zip(pss, CH2):
                    P.op("act", lambda e, pa=pa, st=st, cs=cs, cn=cn: e.activation(out=st[:, cs:cs + cn], in_=pa[:, 0:cn], func=AF.Sigmoid), reads=[pt], writes=[t_st])
                dst, tdst = (GA_scr[m], T_GA[m]) if m < 32 else (GB_scr[m - 32], T_GB[m - 32])
                P.op("sp", lambda e, dst=dst, st=st: e.dma_start(out=dst, in_=st), reads=[t_st], writes=[tdst], dma_sem=s_st)

            ckpt('g')
            sqr = SRing(2, [512], BF16, with_sem=False)
            rstd_q = A.alloc([NT], F32)
            t_rstd_q = P.tok()
            accs = [(psum[6], ptok[6]), (psum[7], ptok[7])]
            for mi in range(8):
                pss = proj(w_in_main, 16 + mi, FT, CH_OWN)
                for ci, ((pa, pt), (cs, cn)) in enumerate(zip(pss, CH2)):
                    P.op("dve", lambda e, pa=pa, mi=mi, cs=cs, cn=cn: e.tensor_copy(out=qn[:, mi, cs:cs + cn], in_=pa[:, 0:cn]), reads=[pt], writes=[t_qn[mi]])
                    sq, t_sq, _ = sqr.next()
                    P.op("act", lambda e, pa=pa, sq=sq, cn=cn: e.activation(out=sq, in_=pa[:, 0:cn], func=AF.Square), reads=[pt], writes=[t_sq])
                    sumsq_accum(sq, t_sq, accs[ci], mi == 0, mi == 7, cn)
            rstd_from_psum(accs, 1024, rstd_q, t_rstd_q, CH2)
            for mi in range(8):
                P.op("dve", lambda e, mi=mi: e.scalar_tensor_tensor(out=qn[:, mi, :], in0=qn[:, mi, :], scalar=cc(C_QN + mi), in1=rstd_q, op0=OP.mult, op1=OP.mult),
                     reads=[t_qn[mi], t_rstd_q, t_cst], writes=[t_qn[mi]])

            ckpt('q')
            uext = A.alloc([NBLK, 144], F32)
            t_u = P.tok()
            sA = A.alloc([NBLK * 144], F32)
            sB = A.alloc([NBLK * 144], F32)
            t_s = P.tok()
            P.op("dve", lambda e: e.memset(sA, 0.0), writes=[t_s])
            P.op("dve", lambda e: e.memset(sB, 0.0), writes=[t_s])
            pooled = SRing(1, [4, NT], BF16, with_sem=False)
            tmp16 = A.alloc([16], F32)
            pmst = SRing(2, [NT], BF16)
            uflat = uext.rearrange("p b t -> p (b t)")
            for g in range(4):
                w = 2 ** (g + 1)
                pl, t_pl, _ = pooled.next()
                for mm in range(4):
                    m = g * 4 + mm
                    pss = proj(w_in_main, m, FT, CH_OWN + [(rhh, 0, 128)])
                    for ci in range(2):
                        pa, pt = pss[ci]
                        plain_copy(evac_eng(), uext[:, 4 * ci:4 * ci + 4, 16:144], pa[:, 0:512].rearrange("p (b t) -> p b t", t=128), [pt], [t_u])
                    pa, pt = pss[2]
                    plain_copy(evac_eng(), uext[:, :, 0:16], pa[:, 0:128].rearrange("p (b t) -> p b t", t=16), [pt], [t_u])
                    cur = uflat
                    bufs = [sA, sB]
                    for k in range(g + 1):
                        sh = 2 ** k
                        nxt = bufs[k % 2]
                        P.op("dve", lambda e, cur=cur, nxt=nxt, sh=sh: e.tensor_tensor(out=nxt[:, sh:], in0=cur[:, sh:], in1=cur[:, 0:NBLK * 144 - sh], op=OP.add), reads=[t_u, t_s], writes=[t_s])
                        cur = nxt
                    s3 = cur.rearrange("p (b t) -> p b t", t=144)
                    P.op("dve", lambda e, s3=s3, pl=pl, mm=mm, w=w: e.scalar_tensor_tensor(out=pl[:, mm, :].rearrange("p (b t) -> p b t", t=128), in0=s3[:, :, 16:144], scalar=1.0 / w, in1=uext[:, :, 16:144], op0=OP.mult, op1=OP.subtract),
                         reads=[t_s, t_u], writes=[t_pl])
                    P.op("dve", lambda e, cur=cur, g=g: e.tensor_tensor(out=tmp16, in0=cur[:, 16:32], in1=cst[:, C_PCORR + g * 16:C_PCORR + (g + 1) * 16], op=OP.mult), reads=[t_s, t_cst], writes=[t_s])
                    P.op("dve", lambda e, pl=pl, mm=mm: e.tensor_tensor(out=pl[:, mm, 0:16], in0=pl[:, mm, 0:16], in1=tmp16, op=OP.add), reads=[t_s, t_pl], writes=[t_pl])
                rp = lambda kt, pl=pl, t_pl=t_pl: (pl[:, kt, :], t_pl)
                for mm in range(4):
                    m = g * 4 + mm
                    pss = proj(w_pool, m, 4, [(rp, 0, 512), (rp, 512, 512)])
                    st, t_st, s_st = pmst.next()
                    for (pa, pt), (cs, cn) in zip(pss, CH2):
                        scale_copy(evac_eng(), st[:, cs:cs + cn], pa[:, 0:cn], cc(C_PSC + m), [pt, t_cst], [t_st])
                    P.op("sp", lambda e, st=st, m=m: e.dma_start(out=PM_scr[m], in_=st), reads=[t_st], writes=[T_PM[m]], dma_sem=s_st)

            P.barrier(dummy[:, 0:1], skip=wtok)
            ckpt(2)

            A.reset(A_BASE2)
            attnT = A.alloc([NH, NT], BF16)
            t_attn = P.toks(NH)
            C_MARK = A.off
            cos_o = A.alloc([NT], F32, 64)
            sin_o = A.alloc([NT], F32, 64)
            t_cs_o = P.tok()
            rtmp = rope_tmp(NT)
            rope_tables(pos_own, NT, cos_o, sin_o, t_cs_o, rtmp)
            P.barrier(dummy[:, 0:1], skip=wtok)
            ckpt(3)
            A.reset(C_MARK + 2 * 4096)
            NKC = 4
            KPC = 64 // NKC
            kring = SRing(5, [S // NKC], BF16)
            vring = SRing(5, [KPC, 128], BF16)
            qnope = SRing(2, [NT], BF16, with_sem=False)
            qrope = SRing(2, [NT], BF16, 64, with_sem=False)
            qz = A.alloc([NT], BF16, 64)
            t_qz = P.tok()
            q1 = A.alloc([512], F32, 64)
            q2 = A.alloc([512], F32, 64)
            t_q12 = P.tok()
            pring = SRing(4, [NT], BF16, with_sem=False)
            daccs = [(A.alloc([NT], F32), P.tok())] * 2
            osb = A.alloc([NT], F32)
            t_osb = P.tok()
            rec = A.alloc([NT], F32)
            t_rec = P.tok()
            maskm = A.alloc([8, 128], BF16)
            t_maskm = P.tok()
            P.op("pool", lambda e: e.dma_start(out=maskm, in_=maskm_d), writes=[t_maskm], dma_sem=sem())
            NRR = 6
            LOOK = 3
            qbufs = {}
            cbufs = {}

            def emit_q(h):
                wq3, t_wq = load_w(w_q[h], 8, 192)
                qn_, t_qn_, _ = qnope.next()
                qr_, t_qr_, _ = qrope.next()
                for ci, (cs, cn) in enumerate(CH2):
                    pa, pt = ps_next(NRR)
                    for kt in range(8):
                        P.op("pe", lambda e, pa=pa, kt=kt, cs=cs, cn=cn, wq3=wq3: e.matmul(pa[:, 0:cn], lhsT=wq3[:, kt, 0:128], rhs=qn[:, kt, cs:cs + cn], start=(kt == 0), stop=(kt == 7)),
                             reads=[t_wq, t_qn[kt]], writes=[pt])
                    plain_copy("dve", qn_[:, cs:cs + cn], pa[:, 0:cn], [pt], [t_qn_])
                    pb, ptb = ps_next(NRR)
                    for kt in range(8):
                        P.op("pe", lambda e, pb=pb, kt=kt, cs=cs, cn=cn, wq3=wq3: e.matmul(pb[0:64, 0:cn], lhsT=wq3[:, kt, 128:192], rhs=qn[:, kt, cs:cs + cn], start=(kt == 0), stop=(kt == 7)),
                             reads=[t_wq, t_qn[kt]], writes=[ptb])
                    P.op("dve", lambda e, pb=pb, cs=cs, cn=cn: e.tensor_copy(out=qz[:, cs:cs + cn], in_=pb[0:64, 0:cn]), reads=[ptb], writes=[t_qz])
                    prot, ptrot = ps_next(NRR)
                    P.op("pe", lambda e, prot=prot, cs=cs, cn=cn: e.matmul(prot[0:64, 0:cn], lhsT=rot_bf[0:64, 0:64], rhs=qz[:, cs:cs + cn], start=True, stop=True), reads=[t_qz, t_mats], writes=[ptrot])
                    P.op("dve", lambda e, cs=cs, cn=cn: e.tensor_tensor(out=q1[:, 0:cn], in0=qz[:, cs:cs + cn], in1=cos_o[:, cs:cs + cn], op=OP.mult), reads=[t_qz, t_cs_o], writes=[t_q12])
                    P.op("dve", lambda e, prot=prot, cs=cs, cn=cn: e.tensor_tensor(out=q2[:, 0:cn], in0=prot[0:64, 0:cn], in1=sin_o[:, cs:cs + cn], op=OP.mult), reads=[ptrot, t_cs_o, t_q12], writes=[t_q12])
                    P.op("dve", lambda e, qr_=qr_, cs=cs, cn=cn: e.tensor_tensor(out=qr_[:, cs:cs + cn], in0=q1[:, 0:cn], in1=q2[:, 0:cn], op=OP.add), reads=[t_q12], writes=[t_qr_])
                qbufs[h] = (qn_, t_qn_, qr_, t_qr_)

            def load_chunk(n):
                h, kc = n // NKC, n % NKC
                kk_, t_kk, s_kk = kring.next()
                vv_, t_vv, s_vv = vring.next()
                kt0 = kc * KPC
                g0 = kc * (NG // NKC)
                P.op("sp", lambda e: e.dma_start(out=kk_, in_=K_scr[h, :, kc * (S // NKC):(kc + 1) * (S // NKC)]),
                     reads=[T_K[h // 4][g] for g in range(g0, g0 + NG // NKC)], writes=[t_kk], dma_sem=s_kk)
                P.op("sp", lambda e: e.dma_start(out=vv_, in_=V_scr[h, kt0:kt0 + KPC, :, :].rearrange("k p d -> p k d")),
                     reads=[T_V[k] for k in range(kt0, kt0 + KPC)], writes=[t_vv], dma_sem=s_vv)
                cbufs[n] = (kk_, t_kk, vv_, t_vv)

            def chunks_of(kb):
                c0 = (kb // 8) * 128
                if c0 < 512:
                    return c0, [(c0, 512 - c0), (512, 512)]
                return c0, [(c0, NT - c0)]

            def phase1(h, kb):
                kk_, t_kk, vv_, t_vv = cbufs[h * NKC + kb // KPC]
                qn_, t_qn_, qr_, t_qr_ = qbufs[h]
                kl = kb % KPC
                cp = kb % 8
                c0, chs = chunks_of(kb)
                pt_, t_pt_, _ = pring.next()
                for (cs, cn) in chs:
                    pa, pt = ps_next(NRR)
                    P.op("pe", lambda e, pa=pa, cs=cs, cn=cn: e.matmul(pa[:, 0:cn], lhsT=kk_[:, kl * 128:(kl + 1) * 128], rhs=qn_[:, cs:cs + cn], start=True, stop=False),
                         reads=[t_kk, t_qn_], writes=[pt])
                    first = (cs == c0)
                    P.op("pe", lambda e, pa=pa, cs=cs, cn=cn, first=first: e.matmul(pa[:, 0:cn], lhsT=kr[:, kb * 128:(kb + 1) * 128], rhs=qr_[:, cs:cs + cn], start=False, stop=(not first)),
                         reads=[t_kr[kb // GA_], t_qr_], writes=[pt])
                    if first:
                        P.op("pe", lambda e, pa=pa: e.matmul(pa[:, 0:128], lhsT=ident_bf, rhs=maskm[:, cp, :], start=False, stop=True), reads=[t_maskm, t_mats], writes=[pt])
                    P.op("act", lambda e, pa=pa, cs=cs, cn=cn: e.activation(out=pt_[:, cs:cs + cn], in_=pa[:, 0:cn], func=AF.Exp, scale=SCALE), reads=[pt], writes=[t_pt_])
                return (pt_, t_pt_)

            first_o = {}

            def phase2(h, kb, st):
                pt_, t_pt_ = st
                kk_, t_kk, vv_, t_vv = cbufs[h * NKC + kb // KPC]
                kl = kb % KPC
                c0, chs = chunks_of(kb)
                ob = 6
                fo = first_o.setdefault(h, [True, True])
                dacc, t_dacc = daccs[h % 2]
                for (cs, cn) in chs:
                    b = cs // 512
                    assert (cs + cn - 1) // 512 == b
                    oa, ot = psum[ob + b], ptok[ob + b]
                    last = (kb == 63) if b == 1 else (kb == 31)
                    P.op("pe", lambda e, oa=oa, cs=cs, cn=cn, b=b, stt=fo[b], last=last: e.matmul(oa[:, cs - b * 512:cs - b * 512 + cn], lhsT=vv_[:, kl, :], rhs=pt_[:, cs:cs + cn], start=stt, stop=last),
                         reads=[t_vv, t_pt_], writes=[ot])
                    fo[b] = False
                if kb == 0:
                    P.op("dve", lambda e: e.tensor_copy(out=dacc, in_=pt_), reads=[t_pt_], writes=[t_dacc])
                else:
                    P.op("dve", lambda e: e.tensor_tensor(out=dacc[:, c0:], in0=dacc[:, c0:], in1=pt_[:, c0:], op=OP.add), reads=[t_pt_, t_dacc], writes=[t_dacc])

            def epilogue(h):
                ob = 6
                dacc, t_dacc = daccs[h % 2]
                for b, (cs, cn) in enumerate(CH2):
                    oa, ot = psum[ob + b], ptok[ob + b]
                    P.op("dve", lambda e, oa=oa, cs=cs, cn=cn: e.tensor_copy(out=osb[:, cs:cs + cn], in_=oa[:, 0:cn]), reads=[ot], writes=[t_osb])
                for b, (cs, cn) in enumerate(CH2):
                    pa, pt = ps_next(NRR)
                    P.op("pe", lambda e, pa=pa, cs=cs, cn=cn: e.matmul(pa[:, 0:cn], lhsT=mats_f[:, 1, :], rhs=dacc[:, cs:cs + cn], start=True, stop=True), reads=[t_dacc, t_mats], writes=[pt])
                    P.op("dve", lambda e, pa=pa, cs=cs, cn=cn: e.reciprocal(out=rec[:, cs:cs + cn], in_=pa[:, 0:cn]), reads=[pt], writes=[t_rec])
                    P.op("dve", lambda e, cs=cs, cn=cn: e.tensor_tensor(out=attnT[:, h, cs:cs + cn], in0=osb[:, cs:cs + cn], in1=rec[:, cs:cs + cn], op=OP.mult), reads=[t_osb, t_rec], writes=[t_attn[h]])

            emit_q(0)
            load_chunk(0)
            load_chunk(1)
            pend = []
            for h in range(NH):
                for kb in range(64):
                    if kb % KPC == 0:
                        n = h * NKC + kb // KPC + 2
                        if n < NH * NKC:
                            load_chunk(n)
                    if kb == 40 and h + 1 < NH:
                        emit_q(h + 1)
                    st = phase1(h, kb)
                    pend.append((h, kb, st))
                    if len(pend) > LOOK:
                        p0 = pend.pop(0)
                        phase2(*p0)
                        if p0[1] == 63:
                            epilogue(p0[0])
            for p0 in pend:
                phase2(*p0)
                if p0[1] == 63:
                    epilogue(p0[0])

            P.barrier(dummy[:, 0:1], skip=wtok)
            ckpt(4)

            A.reset(0)
            pmT = A.alloc([16, NT], BF16)
            t_pmT = P.toks(16)
            assert A.off <= A_BASE2
            A.reset(A_BASE2 + NH * NT * 2)
            mergedT = A.alloc([FT, NT], BF16)
            t_mg = P.toks(FT)
            garing = SRing(2, [NT], BF16)
            gbring = SRing(2, [NT], BF16)
            t1r = SRing(2, [512], F32, with_sem=False)
            t2r = SRing(2, [512], F32, with_sem=False)
            for m in range(16):
                P.op("sp", lambda e, m=m: e.dma_start(out=pmT[:, m, :], in_=PM_scr[m]), reads=[T_PM[m]], writes=[t_pmT[m]], dma_sem=sem())
            rpm = lambda kt: (pmT[:, kt, :], t_pmT[kt])
            rat = lambda kt: (attnT[:, kt, :], t_attn[kt])
            for m in range(FT):
                ga_, t_ga_, s_ga_ = garing.next()
                gb_, t_gb_, s_gb_ = gbring.next()
                P.op("sp", lambda e, ga_=ga_, m=m: e.dma_start(out=ga_, in_=GA_scr[m]), reads=[T_GA[m]], writes=[t_ga_], dma_sem=s_ga_)
                P.op("sp", lambda e, gb_=gb_, m=m: e.dma_start(out=gb_, in_=GB_scr[m]), reads=[T_GB[m]], writes=[t_gb_], dma_sem=s_gb_)
                psa = proj(w_up_pool, m, 16, [(rpm, 0, 512), (rpm, 512, 512)], nrr=8)
                psb = proj(w_up_mla, m, 16, [(rat, 0, 512), (rat, 512, 512)], nrr=8)
                for ci, (cs, cn) in enumerate(CH2):
                    t1, t_t1, _ = t1r.next()
                    t2, t_t2, _ = t2r.next()
                    P.op("dve", lambda e, t1=t1, pa=psa[ci][0], ga_=ga_, cs=cs, cn=cn: e.tensor_tensor(out=t1, in0=pa[:, 0:cn], in1=ga_[:, cs:cs + cn], op=OP.mult), reads=[psa[ci][1], t_ga_], writes=[t_t1])
                    P.op("dve", lambda e, t2=t2, pb=psb[ci][0], gb_=gb_, cs=cs, cn=cn: e.tensor_tensor(out=t2, in0=pb[:, 0:cn], in1=gb_[:, cs:cs + cn], op=OP.mult), reads=[psb[ci][1], t_gb_], writes=[t_t2])
                    P.op("dve", lambda e, t1=t1, t2=t2, m=m, cs=cs, cn=cn: e.tensor_tensor(out=mergedT[:, m, cs:cs + cn], in0=t1, in1=t2, op=OP.add), reads=[t_t1, t_t2], writes=[t_mg[m]])
            P.barrier(dummy[:, 0:1], skip=wtok)
            ckpt(5)

            def out_proj_stage(Wt, KT, rhs_fn, scr, T_scr, base_off, post=None, kchunk=32):
                A.reset(base_off)
                stg = SRing(2, [NT], F32)
                sq_r = SRing(2, [512], BF16, with_sem=False)
                rstd = A.alloc([NT], F32)
                t_rstd = P.tok()
                accs_ = [(psum[6], ptok[6]), (psum[7], ptok[7])]
                for m in range(FT):
                    pss = proj(Wt, m, KT, [(rhs_fn, 0, 512), (rhs_fn, 512, 512)], kchunk=kchunk)
                    st, t_st, s_st = stg.next()
                    for ci, ((pa, pt), (cs, cn)) in enumerate(zip(pss, CH2)):
                        if post is not None:
                            post(m, ci, pa, pt, st[:, cs:cs + cn], t_st)
                            src_ap, src_t = st[:, cs:cs + cn], t_st
                        else:
                            P.op("dve", lambda e, pa=pa, st=st, cs=cs, cn=cn: e.tensor_copy(out=st[:, cs:cs + cn], in_=pa[:, 0:cn]), reads=[pt], writes=[t_st])
                            src_ap, src_t = st[:, cs:cs + cn], t_st
                        sq, t_sq, _ = sq_r.next()
                        P.op("act", lambda e, src_ap=src_ap, sq=sq: e.activation(out=sq, in_=src_ap, func=AF.Square), reads=[src_t], writes=[t_sq])
                        sumsq_accum(sq, t_sq, accs_[ci], m == 0, m == FT - 1, cn)
                    P.op("sp", lambda e, st=st, m=m: e.dma_start(out=scr[m], in_=st), reads=[t_st], writes=[T_scr[m]], dma_sem=s_st)
                rstd_from_psum(accs_, D, rstd, t_rstd, CH2)
                return rstd, t_rstd

            def resid_stage(src_scr, T_src, rstd, t_rstd, g_post_c, resid_kind, resid_scr, T_resid, dst_scr, T_dst, g_pre_c, hnext, t_hnext, final=False):
                srcr = SRing(3, [NT], F32)
                resr = SRing(3, [NT], F32) if resid_kind == "scr" else SRing(3, [NBLK, 128], F32)
                xnr = SRing(3, [NT], F32)
                tmpr = SRing(3, [NT], F32, with_sem=False)
                sq_r = SRing(2, [512], BF16, with_sem=False)
                outr = SRing(2, [NBLK, 128], F32) if final else None
                accs_ = [(psum[6], ptok[6]), (psum[7], ptok[7])]
                loaded = {}

                def issue_loads(m):
                    sr, t_sr, s_sr = srcr.next()
                    P.op("sp", lambda e: e.dma_start(out=sr, in_=src_scr[m]), reads=(T_src[m] if isinstance(T_src[m], list) else [T_src[m]]), writes=[t_sr], dma_sem=s_sr)
                    rr, t_rr, s_rr = resr.next()
                    if resid_kind == "scr":
                        P.op("sp", lambda e: e.dma_start(out=rr, in_=resid_scr[m]), reads=[T_resid[m]], writes=[t_rr], dma_sem=s_rr)
                    else:
                        P.op("sp", lambda e: e.dma_start(out=rr, in_=x_own[:, m * 128:(m + 1) * 128].rearrange("(b p) f -> p b f", p=128)), writes=[t_rr], dma_sem=s_rr)
                    loaded[m] = (sr, t_sr, rr, t_rr)

                issue_loads(0)
                issue_loads(1)
                for m in range(FT):
                    if m + 2 < FT:
                        issue_loads(m + 2)
                    sr, t_sr, rr, t_rr = loaded.pop(m)
                    tm, t_tm, _ = tmpr.next()
                    xn, t_xn, s_xn = xnr.next()
                    P.op("dve", lambda e, tm=tm, sr=sr, m=m: e.scalar_tensor_tensor(out=tm, in0=sr, scalar=cc(g_post_c + m), in1=rstd, op0=OP.mult, op1=OP.mult), reads=[t_sr, t_rstd, t_cst], writes=[t_tm])
                    if resid_kind == "scr":
                        P.op("dve", lambda e, xn=xn, tm=tm, rr=rr: e.tensor_tensor(out=xn, in0=tm, in1=rr, op=OP.add), reads=[t_tm, t_rr], writes=[t_xn])
                    else:
                        for ci, (cs, cn) in enumerate(CH2):
                            pa, pt = ps_next()
                            for bb in range(4):
                                b = ci * 4 + bb
                                P.op("pe", lambda e, pa=pa, rr=rr, b=b, bb=bb: e.transpose(out=pa[:, bb * 128:(bb + 1) * 128], in_=rr[:, b, :], identity=ident_f), reads=[t_rr, t_mats], writes=[pt])
                            P.op("dve", lambda e, xn=xn, tm=tm, pa=pa, cs=cs, cn=cn: e.tensor_tensor(out=xn[:, cs:cs + cn], in0=tm[:, cs:cs + cn], in1=pa[:, 0:cn], op=OP.add), reads=[t_tm, pt], writes=[t_xn])
                    if not final:
                        P.op("sp", lambda e, xn=xn, m=m: e.dma_start(out=dst_scr[m], in_=xn), reads=[t_xn], writes=[T_dst[m]], dma_sem=s_xn)
                        P.op("act", lambda e, xn=xn, m=m: e.activation(out=hnext[:, m, :], in_=xn, func=AF.Copy, scale=cc(g_pre_c + m)), reads=[t_xn, t_cst], writes=[t_hnext[m]])
                        for ci, (cs, cn) in enumerate(CH2):
                            sq, t_sq, _ = sq_r.next()
                            P.op("act", lambda e, xn=xn, sq=sq, cs=cs, cn=cn: e.activation(out=sq, in_=xn[:, cs:cs + cn], func=AF.Square), reads=[t_xn], writes=[t_sq])
                            sumsq_accum(sq, t_sq, accs_[ci], m == 0, m == FT - 1, cn)
                    else:
                        ot_, t_ot, s_ot = outr.next()
                        for ci, (cs, cn) in enumerate(CH2):
                            pa, pt = ps_next()
                            for bb in range(4):
                                b = ci * 4 + bb
                                P.op("pe", lambda e, pa=pa, xn=xn, b=b, bb=bb: e.transpose(out=pa[:, bb * 128:(bb + 1) * 128], in_=xn[:, b * 128:(b + 1) * 128], identity=ident_f), reads=[t_xn, t_mats], writes=[pt])
                            plain_copy(evac_eng(), ot_[:, ci * 4:ci * 4 + 4, :], pa[:, 0:512].rearrange("p (b f) -> p b f", f=128), [pt], [t_ot])
                        P.op("sp", lambda e, ot_=ot_, m=m: e.dma_start(out=out_d[:, m * 128:(m + 1) * 128].rearrange("(b p) f -> p b f", p=128), in_=ot_), reads=[t_ot], writes=[T_OUT[m]], dma_sem=s_ot)
                if not final:
                    rstd2 = A.alloc([NT], F32)
                    t_rstd2 = P.tok()
                    rstd_from_psum(accs_, D, rstd2, t_rstd2, CH2)
                    for m in range(FT):
                        P.op("dve", lambda e, m=m: e.tensor_tensor(out=hnext[:, m, :], in0=hnext[:, m, :], in1=rstd2, op=OP.mult), reads=[t_hnext[m], t_rstd2], writes=[t_hnext[m]])

            MG_END = A_BASE2 + NH * NT * 2 + FT * NT * 2
            rmg = lambda kt: (mergedT[:, kt, :], t_mg[kt])
            rstd1, t_rstd1 = out_proj_stage(w_out, FT, rmg, MIX_scr, T_MIX, MG_END)
            P.barrier(dummy[:, 0:1], skip=wtok)
            ckpt(6)

            A.reset(A_BASE1)
            rstd_keep = A.alloc([NT], F32)
            t_rk = P.tok()
            P.op("dve", lambda e: e.tensor_copy(out=rstd_keep, in_=rstd1), reads=[t_rstd1], writes=[t_rk])
            P.barrier(dummy[:, 0:1], skip=wtok)
            ckpt(7)
            h2 = A.alloc([FT, NT], BF16)
            t_h2 = P.toks(FT)
            H_END = A.off
            resid_stage(MIX_scr, T_MIX, rstd_keep, t_rk, C_GMIXPOST, "x", None, None, X1_scr, T_X1, C_GFFNPRE, h2, t_h2)
            P.barrier(dummy[:, 0:1], skip=wtok)
            ckpt(8)

            A.reset(H_END)
            sgr = SRing(2, [512], F32, with_sem=False)
            actst = SRing(3, [NT], BF16)
            rh2 = lambda kt: (h2[:, kt, :], t_h2[kt])
            for f in range(FFT):
                psg = proj(w_fg, f, FT, [(rh2, 0, 512), (rh2, 512, 512)], nrr=8)
                psu = proj(w_fu, f, FT, [(rh2, 0, 512), (rh2, 512, 512)], nrr=8)
                st, t_st, s_st = actst.next()
                for ci, (cs, cn) in enumerate(CH2):
                    sg, t_sg, _ = sgr.next()
                    P.op("act", lambda e, sg=sg, pa=psg[ci][0]: e.activation(out=sg, in_=pa[:, 0:512], func=AF.Silu), reads=[psg[ci][1]], writes=[t_sg])
                    P.op("dve", lambda e, sg=sg, pu=psu[ci][0], st=st, cs=cs, cn=cn: e.tensor_tensor(out=st[:, cs:cs + cn], in0=pu[:, 0:cn], in1=sg, op=OP.mult), reads=[psu[ci][1], t_sg], writes=[t_st])
                P.op("sp", lambda e, st=st, f=f: e.dma_start(out=ACT_scr[f], in_=st), reads=[t_st], writes=[T_ACT[f]], dma_sem=s_st)
            P.barrier(dummy[:, 0:1], skip=wtok)
            ckpt(9)

            A.reset(A_BASE1 + NT * 4)
            actT = A.alloc([FFT, 512], BF16)
            t_actT = P.toks(FFT)
            stg = SRing(2, [512], F32)
            sq_r = SRing(2, [512], BF16, with_sem=False)
            rstd_f = A.alloc([NT], F32)
            t_rstd_f = P.tok()
            accs = [(psum[6], ptok[6]), (psum[7], ptok[7])]
            for half in range(2):
                for f0 in range(0, FFT, 8):
                    f1 = min(FFT, f0 + 8)
                    s_ = sem()
                    P.op("sp", lambda e, f0=f0, f1=f1, half=half: e.dma_start(out=actT[:, f0:f1, :], in_=ACT_scr[f0:f1, :, half * 512:(half + 1) * 512].rearrange("f p t -> p f t")),
                         reads=[T_ACT[f] for f in range(f0, f1)], writes=[t_actT[f] for f in range(f0, f1)], dma_sem=s_)
                ract = lambda kt: (actT[:, kt, :], t_actT[kt])
                for m in range(FT):
                    pss = proj(w_fd, m, FFT, [(ract, 0, 512)])
                    pa, pt = pss[0]
                    st, t_st, s_st = stg.next()
                    P.op("dve", lambda e, pa=pa, st=st: e.tensor_copy(out=st, in_=pa[:, 0:512]), reads=[pt], writes=[t_st])
                    sq, t_sq, _ = sq_r.next()
                    P.op("act", lambda e, st=st, sq=sq: e.activation(out=sq, in_=st, func=AF.Square), reads=[t_st], writes=[t_sq])
                    sumsq_accum(sq, t_sq, accs[half], m == 0, m == FT - 1, 512)
                    P.op("sp", lambda e, st=st, m=m, half=half: e.dma_start(out=FFN_scr[m, :, half * 512:(half + 1) * 512], in_=st), reads=[t_st], writes=[T_FFN[m][half]], dma_sem=s_st)
            rstd_from_psum(accs, D, rstd_f, t_rstd_f, CH2)
            P.barrier(dummy[:, 0:1], skip=wtok)
            ckpt(10)

            A.reset(A_BASE1)
            rstd_keep = A.alloc([NT], F32)
            P.op("dve", lambda e: e.tensor_copy(out=rstd_keep, in_=rstd_f), reads=[t_rstd_f], writes=[t_rk])
            P.barrier(dummy[:, 0:1], skip=wtok)
            ckpt(11)
            h3 = A.alloc([FT, NT], BF16)
            t_h3 = P.toks(FT)
            H_END = A.off
            resid_stage(FFN_scr, T_FFN, rstd_keep, t_rk, C_GFFNPOST, "scr", X1_scr, T_X1, X2_scr, T_X2, C_GPLEPRE, h3, t_h3)
            P.barrier(dummy[:, 0:1], skip=wtok)
            ckpt(12)

            A.reset(H_END)
            pT = A.alloc([2, NT], BF16)
            t_pT = P.toks(2)
            pring_ = SRing(4, [256], BF16)
            for gg in range(2):
                front_end(lambda i, gg=gg: p_own[(gg * 4 + i) * 128:(gg * 4 + i + 1) * 128, :], 4, pring_, None, None, None, None, None, pT, t_pT, gg * 512, KT=2, width=256)
            rh3 = lambda kt: (h3[:, kt, :], t_h3[kt])
            rpT = lambda kt: (pT[:, kt, :], t_pT[kt])
            sgr = SRing(2, [512], F32, with_sem=False)

            def ple_post_factory():
                state = {}

                def post(m, ci, pa, pt, dst, t_dst):
                    if ci == 0:
                        state["pp"] = proj(w_pp, m, 2, [(rpT, 0, 512), (rpT, 512, 512)], nrr=6)
                    sg, t_sg, _ = sgr.next()
                    P.op("act", lambda e, sg=sg, pa=pa: e.activation(out=sg, in_=pa[:, 0:512], func=AF.Sigmoid), reads=[pt], writes=[t_sg])
                    pp, ppt = state["pp"][ci]
                    P.op("dve", lambda e, sg=sg, pp=pp, dst=dst: e.tensor_tensor(out=dst, in0=pp[:, 0:512], in1=sg, op=OP.mult), reads=[ppt, t_sg], writes=[t_dst])
                return post

            PLE_BASE = A.off
            rstd_p, t_rstd_p = out_proj_stage(w_pg, FT, rh3, PRD_scr, T_PRD, PLE_BASE, post=ple_post_factory())
            P.barrier(dummy[:, 0:1], skip=wtok)
            ckpt(13)
            A.reset(A_BASE1)
            rstd_keep = A.alloc([NT], F32)
            P.op("dve", lambda e: e.tensor_copy(out=rstd_keep, in_=rstd_p), reads=[t_rstd_p], writes=[t_rk])
            P.barrier(dummy[:, 0:1], skip=wtok)
            ckpt(14)
            resid_stage(PRD_scr, T_PRD, rstd_keep, t_rk, C_GPLEPOST, "scr", X2_scr, T_X2, None, None, None, None, None, final=True)
        try:
            body()
        except _Stop:
            pass
        P.op("sp", lambda e: None, reads=T_OUT)
        P.finalize(nc, es)
    return nc


def _tile_w(W, KT, MT):
    return np.ascontiguousarray(W.reshape(KT, 128, MT, 128).transpose(2, 1, 0, 3))


_NC_CACHE = {}


def _prep(x, p, positions, norm_mix_pre, norm_mix_post, w_in, q_norm, kv_norm, w_q_b, w_kv_b,
           w_pool, pool_scale, w_up_pool, w_up_mla, w_branch_gate, w_out, norm_ffn_pre,
           norm_ffn_post, w_ffn_gate, w_ffn_up, w_ffn_down, norm_ple_pre, w_ple_gate,
           w_ple_proj, norm_ple_post):
    f32 = np.float32
    x = np.asarray(x, f32)[0]
    p = np.asarray(p, f32)[0, 0]
    pos = np.asarray(positions, np.int32)[0]
    w_in = np.asarray(w_in, f32)[0]

    shared = {}
    shared["x_all"] = x
    shared["pos_all"] = np.ascontiguousarray(np.broadcast_to(pos[None, :], (64, S)))
    shared["w_in_main"] = _tile_w(w_in[:, 0:3072], 32, 24)
    shared["w_in_kv"] = np.ascontiguousarray(w_in[:, 3072:3648].reshape(32, 128, 576).transpose(1, 0, 2))
    wkv = np.asarray(w_kv_b, f32)[0]
    shared["w_k"] = np.ascontiguousarray(wkv[:, :, 0:128].reshape(4, 128, NH * 128).transpose(1, 0, 2))
    shared["w_v"] = np.ascontiguousarray(wkv[:, :, 128:256].reshape(4, 128, NH * 128).transpose(1, 0, 2))
    wq = np.asarray(w_q_b, f32)[0]
    shared["w_q"] = np.ascontiguousarray(wq.reshape(8, 128, NH, 192).transpose(2, 1, 0, 3))
    wp = np.asarray(w_pool, f32)[0]
    shared["w_pool"] = np.ascontiguousarray(wp.reshape(4, 4, 128, 4, 128).transpose(0, 3, 2, 1, 4).reshape(16, 128, 4, 128))
    shared["w_up_pool"] = _tile_w(np.asarray(w_up_pool, f32)[0], 16, 32)
    shared["w_up_mla"] = _tile_w(np.asarray(w_up_mla, f32)[0], 16, 32)
    shared["w_gate"] = _tile_w(np.asarray(w_branch_gate, f32)[0].reshape(D, 2 * D), 32, 64)
    shared["w_out"] = _tile_w(np.asarray(w_out, f32)[0], 32, 32)
    shared["w_fg"] = _tile_w(np.asarray(w_ffn_gate, f32)[0], 32, FFT)
    shared["w_fu"] = _tile_w(np.asarray(w_ffn_up, f32)[0], 32, FFT)
    shared["w_fd"] = _tile_w(np.asarray(w_ffn_down, f32)[0], FFT, 32)
    shared["w_pg"] = _tile_w(np.asarray(w_ple_gate, f32)[0], 32, 32)
    shared["w_pp"] = _tile_w(np.asarray(w_ple_proj, f32)[0], 2, 32)
    mats = np.zeros((128, 3, 128), f32)
    mats[:, 0, :] = np.eye(128, dtype=f32)
    mats[:, 1, :] = 1.0
    for j in range(32):
        mats[32 + j, 2, j] = -1.0
        mats[j, 2, 32 + j] = 1.0
    shared["mats"] = mats

    def colmajor(v, nt):
        return np.asarray(v, f32).reshape(nt, 128).T

    cst0 = np.zeros((128, C_NCOL), f32)
    cst0[:, C_GMIXPRE:C_GMIXPRE + 32] = colmajor(norm_mix_pre[0], 32)
    cst0[:, C_GMIXPOST:C_GMIXPOST + 32] = colmajor(norm_mix_post[0], 32)
    cst0[:, C_GFFNPRE:C_GFFNPRE + 32] = colmajor(norm_ffn_pre[0], 32)
    cst0[:, C_GFFNPOST:C_GFFNPOST + 32] = colmajor(norm_ffn_post[0], 32)
    cst0[:, C_GPLEPRE:C_GPLEPRE + 32] = colmajor(norm_ple_pre[0], 32)
    cst0[:, C_GPLEPOST:C_GPLEPOST + 32] = colmajor(norm_ple_post[0], 32)
    cst0[:, C_QN:C_QN + 8] = colmajor(q_norm[0], 8)
    cst0[:, C_KVN:C_KVN + 4] = colmajor(kv_norm[0], 4)
    cst0[:, C_PSC:C_PSC + 16] = colmajor(pool_scale[0], 16)
    invf = (10000.0 ** (-np.arange(0, 64, 2, dtype=f32) / 64)).astype(f32)
    cst0[0:32, C_INVF] = invf
    cst0[32:64, C_INVF] = invf
    cst0[:, C_EPS] = EPS

    xb = x.reshape(8, 8, 128, D)
    pb = p.reshape(8, 8, 128, 256)
    posb = pos.reshape(8, 8, 128)
    in_maps = []
    for c in range(NCORE):
        m = dict(shared)
        m["x_own"] = np.ascontiguousarray(xb[:, c]).reshape(NT, D)
        m["p_own"] = np.ascontiguousarray(pb[:, c]).reshape(NT, 256)
        m["pos_own"] = np.ascontiguousarray(np.broadcast_to(posb[:, c].reshape(1, NT), (64, NT)))
        halo = np.zeros((8, 16, D), f32)
        for i in range(8):
            t0 = (8 * i + c) * 128
            if t0 >= 16:
                halo[i] = x[t0 - 16:t0]
        m["x_halo"] = halo.reshape(128, D)
        cst = cst0.copy()
        for cp in range(8):
            for half in range(2):
                col = C_MASK + cp * 2 + half
                if cp > c:
                    cst[:, col] = NEG
                elif cp == c and half == 0:
                    cst[64:128, col] = NEG
        if c == 0:
            for g in range(4):
                w = 2 ** (g + 1)
                for t in range(16):
                    cst[:, C_PCORR + g * 16 + t] = 1.0 / min(w, t + 1) - 1.0 / w
        m["cst"] = cst
        mk = np.zeros((128, 8, 128), f32)
        for cp in range(8):
            if cp > c:
                mk[:, cp, :] = NEG
            elif cp == c:
                mk[64:128, cp, 0:64] = NEG
        m["maskm"] = mk
        in_maps.append(m)

    return in_maps


def kernel(x, p, positions, norm_mix_pre, norm_mix_post, w_in, q_norm, kv_norm, w_q_b, w_kv_b,
           w_pool, pool_scale, w_up_pool, w_up_mla, w_branch_gate, w_out, norm_ffn_pre,
           norm_ffn_post, w_ffn_gate, w_ffn_up, w_ffn_down, norm_ple_pre, w_ple_gate,
           w_ple_proj, norm_ple_post):
    in_maps = _prep(x, p, positions, norm_mix_pre, norm_mix_post, w_in, q_norm, kv_norm, w_q_b, w_kv_b,
                    w_pool, pool_scale, w_up_pool, w_up_mla, w_branch_gate, w_out, norm_ffn_pre,
                    norm_ffn_post, w_ffn_gate, w_ffn_up, w_ffn_down, norm_ple_pre, w_ple_gate,
                    w_ple_proj, norm_ple_post)
    f32 = np.float32
    if "nc" not in _NC_CACHE:
        _NC_CACHE["nc"] = build_program()
    nc = _NC_CACHE["nc"]
    res = run_bass_kernel_spmd(nc, in_maps, core_ids=list(range(NCORE)))
    out = np.empty((8, 8, 128, D), f32)
    for c in range(NCORE):
        out[:, c] = res.results[c]["out"].reshape(8, 128, D)
    return out.reshape(1, S, D)
```

```python
import numpy as np
import concourse.bass as bass
import concourse.mybir as mybir
from concourse.bass_utils import run_bass_kernel_spmd
from contextlib import ExitStack

F32, BF16, I32 = mybir.dt.float32, mybir.dt.bfloat16, mybir.dt.int32
AF = mybir.ActivationFunctionType
OP = mybir.AluOpType

NCORE = 8
S = 8192
D = 4096
NT = 1024
NBLK = 8
FT = 32
DFF = 11008
FFT = 86
NH = 16
EPS = 1e-6
SCALE = 192 ** -0.5
NEG = -30000.0
PI = float(np.pi)

C_GMIXPRE, C_GMIXPOST, C_GFFNPRE, C_GFFNPOST, C_GPLEPRE, C_GPLEPOST = 0, 32, 64, 96, 128, 160
C_QN, C_KVN, C_PSC, C_INVF, C_EPS, C_MASK, C_PCORR, C_NCOL = 192, 200, 204, 220, 221, 222, 238, 304


class Tok:
    __slots__ = ("w", "rs", "excl")

    def __init__(self):
        self.w = None
        self.rs = []
        self.excl = False


class Op:
    __slots__ = ("eng", "fn", "deps", "signal", "count", "sem", "val", "is_dma")


class Prog:
    ENG = ("pe", "act", "dve", "pool", "sp")

    def __init__(self):
        self.streams = {e: [] for e in self.ENG}
        self.dma_val = {}
        self.all_toks = []
        self.last_barrier = None

    def tok(self):
        t = Tok()
        t.w = self.last_barrier
        self.all_toks.append(t)
        return t

    def toks(self, n):
        return [self.tok() for _ in range(n)]

    def op(self, eng, fn, reads=(), writes=(), dma_sem=None):
        o = Op()
        o.eng = eng
        o.fn = fn
        o.signal = False
        o.count = 0
        o.is_dma = dma_sem is not None
        o.sem = dma_sem
        o.val = 0
        if o.is_dma:
            k = id(dma_sem)
            self.dma_val[k] = self.dma_val.get(k, 0) + 16
            o.val = self.dma_val[k]
        if any(t.excl for t in reads):
            writes = list(writes) + [t for t in reads if t.excl]
            reads = [t for t in reads if not t.excl]
        deps = []
        seen = set()

        def add(d):
            if d is None or id(d) in seen:
                return
            seen.add(id(d))
            if (not d.is_dma) and (not o.is_dma) and d.eng == eng and eng == "pe":
                return
            deps.append(d)
            if not d.is_dma:
                d.signal = True

        for t in reads:
            add(t.w)
        for t in writes:
            add(t.w)
            for r in t.rs:
                add(r)
        o.deps = deps
        for t in reads:
            if not o.is_dma:
                t.rs = [r for r in t.rs if r.is_dma or r.eng != eng]
            t.rs.append(o)
        for t in writes:
            t.w = o
            t.rs = []
        self.streams[eng].append(o)
        return o

    def barrier(self, dummy_ap, skip=()):
        sk = set(id(t) for t in skip)
        self.last_barrier = self.op("dve", lambda e: e.memset(dummy_ap, 0.0), writes=[t for t in self.all_toks if id(t) not in sk])

    def finalize(self, nc, es):
        esem = {e: es.enter_context(nc.semaphore("es_" + e)) for e in ("pe", "act", "dve", "pool")}
        for e in self.ENG:
            cnt = 0
            for o in self.streams[e]:
                if (not o.is_dma) and o.signal:
                    cnt += 1
                    o.count = cnt
        block = es.enter_context(nc.Block())
        streams = self.streams

        def run(ename):
            def body(eng):
                clock = {}
                for o in streams[ename]:
                    for d in o.deps:
                        if d.is_dma:
                            key = ("d", id(d.sem))
                            sem = d.sem
                            val = d.val
                        else:
                            key = d.eng
                            sem = esem[d.eng]
                            val = d.count
                        if clock.get(key, 0) < val:
                            eng.wait_ge(sem, val)
                            clock[key] = val
                    ins = o.fn(eng)
                    if o.is_dma:
                        ins.then_inc(o.sem, 16)
                    elif o.signal:
                        ins.then_inc(esem[ename], 1)
            return body

        block.tensor(run("pe"))
        block.scalar(run("act"))
        block.vector(run("dve"))
        block.gpsimd(run("pool"))
        block.sync(run("sp"))


class Arena:
    def __init__(self, t, nbytes):
        self.t = t
        self.n = nbytes
        self.off = 0

    def reset(self, off=0):
        self.off = off

    def alloc(self, shape, dtype, parts=128):
        esz = 4 if dtype in (F32, I32) else 2
        n = 1
        for s in shape:
            n *= s
        nb = (n * esz + 63) // 64 * 64
        assert self.off + nb <= self.n, ("arena overflow", self.off, nb, self.n)
        a = self.t[0:parts, self.off // 2:(self.off + n * esz) // 2]
        self.off += nb
        if dtype != BF16:
            a = a.bitcast(dtype)
        if len(shape) == 2:
            a = a.rearrange("p (a b) -> p a b", b=shape[1])
        elif len(shape) == 3:
            a = a.rearrange("p (a b c) -> p a b c", b=shape[1], c=shape[2])
        return a


def build_program(stop_after=None, debug=False):
    nc = bass.Bass("TRN2", target_bir_lowering=False)
    P = Prog()

    def din(name, shape, dt=F32):
        return nc.dram_tensor(name, list(shape), dt, kind="ExternalInput").ap()

    def dscr(name, shape, dt):
        return nc.dram_tensor(name, list(shape), dt, kind=("ExternalOutput" if debug else "Internal")).ap()

    x_all = din("x_all", [S, D])
    x_own = din("x_own", [NT, D])
    x_halo = din("x_halo", [128, D])
    p_own = din("p_own", [NT, 256])
    pos_all = din("pos_all", [64, S], I32)
    pos_own = din("pos_own", [64, NT], I32)
    cst_d = din("cst", [128, C_NCOL])
    mats_d = din("mats", [128, 3, 128])
    maskm_d = din("maskm", [128, 8, 128])
    w_in_main = din("w_in_main", [24, 128, 32, 128])
    w_in_kv = din("w_in_kv", [128, 32, 576])
    w_k = din("w_k", [128, 4, NH * 128])
    w_v = din("w_v", [128, 4, NH * 128])
    w_q = din("w_q", [NH, 128, 8, 192])
    w_pool = din("w_pool", [16, 128, 4, 128])
    w_up_pool = din("w_up_pool", [32, 128, 16, 128])
    w_up_mla = din("w_up_mla", [32, 128, 16, 128])
    w_gate = din("w_gate", [64, 128, 32, 128])
    w_out = din("w_out", [32, 128, 32, 128])
    w_fg = din("w_fg", [FFT, 128, 32, 128])
    w_fu = din("w_fu", [FFT, 128, 32, 128])
    w_fd = din("w_fd", [32, 128, FFT, 128])
    w_pg = din("w_pg", [32, 128, 32, 128])
    w_pp = din("w_pp", [32, 128, 2, 128])
    out_d = nc.dram_tensor("out", [NT, D], F32, kind="ExternalOutput").ap()

    K_scr = dscr("K_scr", [NH, 128, S], BF16)
    V_scr = dscr("V_scr", [NH, 64, 128, 128], BF16)
    KVN_scr = dscr("KVN_scr", [4, 128, S], BF16)
    T_KVN = P.toks(16)
    GA_scr = dscr("GA_scr", [32, 128, NT], BF16)
    GB_scr = dscr("GB_scr", [32, 128, NT], BF16)
    PM_scr = dscr("PM_scr", [16, 128, NT], BF16)
    MIX_scr = dscr("MIX_scr", [32, 128, NT], F32)
    X1_scr = dscr("X1_scr", [32, 128, NT], F32)
    X2_scr = dscr("X2_scr", [32, 128, NT], F32)
    ACT_scr = dscr("ACT_scr", [FFT, 128, NT], BF16)
    FFN_scr = dscr("FFN_scr", [32, 128, NT], F32)
    PRD_scr = dscr("PRD_scr", [32, 128, NT], F32)
    T_K = [[P.tok() for _ in range(32)] for _ in range(4)]
    T_V = [P.tok() for _ in range(64)]
    T_GA, T_GB = P.toks(32), P.toks(32)
    T_PM = P.toks(16)
    T_MIX, T_X1, T_X2, T_FFN, T_PRD = P.toks(32), P.toks(32), P.toks(32), [P.toks(2) for _ in range(32)], P.toks(32)
    T_ACT = P.toks(FFT)
    T_OUT = P.toks(32)

    with ExitStack() as es:
        ARENA_BYTES = 148 * 1024
        NW = 6
        FULL_BYTES = ARENA_BYTES + NW * 8192
        arena_t = es.enter_context(nc.sbuf_tensor("arena", [128, FULL_BYTES // 2], BF16))
        A = Arena(arena_t, ARENA_BYTES)
        cst = es.enter_context(nc.sbuf_tensor("cst_sb", [128, C_NCOL], F32))
        mats_bf = es.enter_context(nc.sbuf_tensor("mats_bf", [128, 3, 128], BF16))
        mats_f = es.enter_context(nc.sbuf_tensor("mats_f", [128, 2, 128], F32))
        dummy = es.enter_context(nc.sbuf_tensor("dummy_sb", [128, 8], F32))
        wring = [arena_t[:, ARENA_BYTES // 2 + i * 4096:ARENA_BYTES // 2 + (i + 1) * 4096] for i in range(NW)]
        wtok = P.toks(NW)
        wsem = [es.enter_context(nc.semaphore(f"wsem{i}")) for i in range(NW)]
        wstate = {"i": 0}
        psum = [es.enter_context(nc.psum_tensor(f"ps{i}", [128, 512], F32)) for i in range(8)]
        ptok = P.toks(8)
        for t_ in ptok:
            t_.excl = True
        pstate = {"i": 0}
        NSEM = 84
        gsem = [es.enter_context(nc.semaphore(f"gsem{i}")) for i in range(NSEM)]
        gstate = {"i": 0}
        t_cst, t_mats = P.tok(), P.tok()
        rope_sem = es.enter_context(nc.semaphore("rope_sem"))

        ident_bf = mats_bf[:, 0, :]
        ones_bf = mats_bf[:, 1, :]
        rot_bf = mats_bf[:, 2, :]
        ident_f = mats_f[:, 0, :]

        def sem():
            s_ = gsem[gstate["i"] % NSEM]
            gstate["i"] += 1
            return s_

        def ps_next(nrr=6):
            k = pstate["i"] % nrr
            pstate["i"] += 1
            return psum[k], ptok[k]

        def wslot():
            k = wstate["i"] % NW
            wstate["i"] += 1
            return wring[k], wtok[k], wsem[k]

        class SRing:
            def __init__(self, n, shape, dtype, parts=128, with_sem=True):
                self.aps = [A.alloc(shape, dtype, parts) for _ in range(n)]
                self.tk = P.toks(n)
                self.sm = [sem() for _ in range(n)] if with_sem else [None] * n
                self.i = 0
                self.n = n

            def next(self):
                k = self.i % self.n
                self.i += 1
                return self.aps[k], self.tk[k], self.sm[k]

        cc = lambda c: cst[:, c:c + 1]

        P.op("sp", lambda e: e.dma_start(out=cst[:], in_=cst_d), writes=[t_cst], dma_sem=sem())
        P.op("pool", lambda e: e.dma_start(out=mats_bf[:], in_=mats_d), writes=[t_mats], dma_sem=sem())
        P.op("sp", lambda e: e.dma_start(out=mats_f[:], in_=mats_d[:, 0:2, :]), writes=[t_mats], dma_sem=sem())

        evq = {"i": 0}

        def evac_eng():
            evq["i"] += 1
            return "act" if evq["i"] % 2 else "dve"

        def scale_copy(eng, out, in_, scale_ap, reads, writes):
            if eng == "act":
                P.op("act", lambda e: e.activation(out=out, in_=in_, func=AF.Copy, scale=scale_ap), reads=reads, writes=writes)
            else:
                P.op("dve", lambda e: e.tensor_scalar(out=out, in0=in_, scalar1=scale_ap, scalar2=None, op0=OP.mult), reads=reads, writes=writes)

        def plain_copy(eng, out, in_, reads, writes):
            if eng == "act":
                P.op("act", lambda e: e.activation(out=out, in_=in_, func=AF.Copy), reads=reads, writes=writes)
            else:
                P.op("dve", lambda e: e.tensor_copy(out=out, in_=in_), reads=reads, writes=writes)

        def rstd_from_psum(ps_list, n_feat, rstd_ap, rstd_tok, chunks):
            for (pa, pt), (cs, cn) in zip(ps_list, chunks):
                P.op("act", lambda e, pa=pa, cs=cs, cn=cn: e.activation(out=rstd_ap[:, cs:cs + cn], in_=pa[:, 0:cn], func=AF.Sqrt, bias=cc(C_EPS), scale=1.0 / n_feat),
                     reads=[pt, t_cst], writes=[rstd_tok])
            P.op("dve", lambda e: e.reciprocal(out=rstd_ap, in_=rstd_ap), reads=[rstd_tok], writes=[rstd_tok])

        def rope_tables(pos_ap, n, cos_ap, sin_ap, t_out, tmp):
            pi_t, pf, kf, ki, ang, t_tmp = tmp
            P.op("sp", lambda e: e.dma_start(out=pi_t, in_=pos_ap), writes=[t_tmp], dma_sem=rope_sem)
            P.op("dve", lambda e: e.tensor_copy(out=pf, in_=pi_t), reads=[t_tmp], writes=[t_tmp])
            P.op("dve", lambda e: e.tensor_scalar(out=pf, in0=pf, scalar1=cst[0:64, C_INVF:C_INVF + 1], scalar2=None, op0=OP.mult), reads=[t_tmp, t_cst], writes=[t_tmp])
            for dst, shift in ((sin_ap, 0.0), (cos_ap, PI / 2)):
                P.op("dve", lambda e, shift=shift: e.tensor_scalar(out=ki, in0=pf, scalar1=shift, scalar2=float(1 / (2 * PI)), op0=OP.add, op1=OP.mult), reads=[t_tmp], writes=[t_tmp])
                P.op("dve", lambda e: e.tensor_copy(out=kf, in_=ki), reads=[t_tmp], writes=[t_tmp])
                P.op("dve", lambda e: e.scalar_tensor_tensor(out=ang, in0=kf, scalar=float(-2 * PI), in1=pf, op0=OP.mult, op1=OP.add), reads=[t_tmp], writes=[t_tmp])
                if shift != 0.0:
                    P.op("dve", lambda e, shift=shift: e.tensor_scalar(out=ang, in0=ang, scalar1=shift, scalar2=None, op0=OP.add), reads=[t_tmp], writes=[t_tmp])
                P.op("dve", lambda e: e.tensor_scalar(out=kf, in0=ang, scalar1=PI, scalar2=float(-2 * PI), op0=OP.is_gt, op1=OP.mult), reads=[t_tmp], writes=[t_tmp])
                P.op("dve", lambda e: e.tensor_tensor(out=ang, in0=ang, in1=kf, op=OP.add), reads=[t_tmp], writes=[t_tmp])
                P.op("dve", lambda e: e.tensor_scalar(out=kf, in0=ang, scalar1=-PI, scalar2=float(2 * PI), op0=OP.is_lt, op1=OP.mult), reads=[t_tmp], writes=[t_tmp])
                P.op("dve", lambda e: e.tensor_tensor(out=ang, in0=ang, in1=kf, op=OP.add), reads=[t_tmp], writes=[t_tmp])
                P.op("act", lambda e, dst=dst: e.activation(out=dst, in_=ang, func=AF.Sin), reads=[t_tmp], writes=[t_out])

        def rope_tmp(n):
            return (A.alloc([n], I32, 64), A.alloc([n], F32, 64), A.alloc([n], F32, 64), A.alloc([n], I32, 64), A.alloc([n], F32, 64), P.tok())

        def front_end(rows_fn, nblk, xring, junk, t_junk, ssb, t_ss, gain_c0, hT, hT_tok, col0, KT=FT, width=D):
            xbs = fe_load(rows_fn, nblk, xring, junk, t_junk, ssb, t_ss, gain_c0, width)
            fe_transpose(xbs, nblk, gain_c0, hT, hT_tok, col0, KT)

        def fe_load(rows_fn, nblk, xring, junk, t_junk, ssb, t_ss, gain_c0, width=D):
            xbs = []
            for i in range(nblk):
                xb, t_xb, s_xb = xring.next()
                P.op("pool", lambda e, xb=xb, i=i: e.dma_start(out=xb, in_=rows_fn(i)), writes=[t_xb], dma_sem=s_xb)
                if gain_c0 is not None:
                    P.op("act", lambda e, xb=xb, i=i: e.activation(out=junk, in_=xb, func=AF.Square, accum_out=ssb[:, i:i + 1]), reads=[t_xb], writes=[t_junk, t_ss[i]])
                    P.op("act", lambda e, i=i: e.activation(out=ssb[:, i:i + 1], in_=ssb[:, i:i + 1], func=AF.Sqrt, bias=cc(C_EPS), scale=1.0 / width), reads=[t_ss[i], t_cst], writes=[t_ss[i]])
                    P.op("dve", lambda e, i=i: e.reciprocal(out=ssb[:, i:i + 1], in_=ssb[:, i:i + 1]), reads=[t_ss[i]], writes=[t_ss[i]])
                    P.op("dve", lambda e, xb=xb, i=i: e.tensor_scalar(out=xb, in0=xb, scalar1=ssb[:, i:i + 1], scalar2=None, op0=OP.mult), reads=[t_ss[i], t_xb], writes=[t_xb])
                xbs.append((xb, t_xb))
            return xbs

        def fe_transpose(xbs, nblk, gain_c0, hT, hT_tok, col0, KT=FT):
            for ft in range(KT):
                pa, pt = ps_next()
                pab = pa[:].bitcast(BF16)
                for i, (xb, t_xb) in enumerate(xbs):
                    P.op("pe", lambda e, pab=pab, xb=xb, i=i, ft=ft: e.transpose(out=pab[:, i * 128:(i + 1) * 128], in_=xb[:, ft * 128:(ft + 1) * 128], identity=ident_bf),
                         reads=[t_xb, t_mats], writes=[pt])
                eng = evac_eng()
                o_ap = hT[:, ft, col0:col0 + nblk * 128]
                i_ap = pab[:, 0:nblk * 128]
                if gain_c0 is not None:
                    scale_copy(eng, o_ap, i_ap, cc(gain_c0 + ft), [pt, t_cst], [hT_tok[ft]])
                else:
                    plain_copy(eng, o_ap, i_ap, [pt], [hT_tok[ft]])

        def load_w(src_ap, kc, n=128):
            wt, t_w, s_w = wslot()
            wv_ = wt[:, 0:kc * n].rearrange("p (k n) -> p k n", n=n)
            P.op("pool", lambda e: e.dma_start(out=wv_, in_=src_ap), writes=[t_w], dma_sem=s_w)
            return wv_, t_w

        def proj(Wt, m, KT, chunks, nrr=6, kchunk=32, mcols=128):
            pss = [ps_next(nrr) for _ in chunks]
            k0 = 0
            while k0 < KT:
                kc = min(kchunk, KT - k0)
                wv_, t_w = load_w(Wt[m, :, k0:k0 + kc, :], kc)
                for kk in range(kc):
                    kt = k0 + kk
                    for (pa, pt), (rfn, cs, cn) in zip(pss, chunks):
                        ra, rt = rfn(kt)
                        P.op("pe", lambda e, pa=pa, wv_=wv_, kk=kk, ra=ra, cs=cs, cn=cn, kt=kt: e.matmul(pa[0:mcols, 0:cn], lhsT=wv_[:, kk, 0:mcols], rhs=ra[:, cs:cs + cn], start=(kt == 0), stop=(kt == KT - 1)),
                             reads=[t_w, rt], writes=[pt])
                k0 += kc
            return pss

        CH2 = [(0, 512), (512, 512)]

        def sumsq_accum(sq_ap, t_sq, acc, first, last, cn):
            pa, pt = acc
            P.op("pe", lambda e: e.matmul(pa[:, 0:cn], lhsT=ones_bf, rhs=sq_ap, start=first, stop=last), reads=[t_sq, t_mats], writes=[pt])

        class _Stop(Exception):
            pass

        def ckpt(name):
            if stop_after == name:
                raise _Stop()

        def body():
            A.n = FULL_BYTES
            A.reset(0)
            kr = A.alloc([S], BF16, 64)
            GA_ = 4
            NG = S // (GA_ * 128)
            GT = GA_ * 128
            t_kr = P.toks(NG)
            A_BASE1 = A.off
            wkvin = A.alloc([32, 576], BF16)
            t_wres = P.tok()
            P.op("pool", lambda e: e.dma_start(out=wkvin, in_=w_in_kv), writes=[t_wres], dma_sem=sem())
            xring = SRing(7, [D], BF16)
            junk = A.alloc([D], BF16)
            t_junk = P.tok()
            ssb = A.alloc([8], F32)
            t_ss = P.toks(8)
            hTa = A.alloc([FT, GT], BF16)
            t_hTa = P.toks(FT)
            kvl = A.alloc([4, GT], F32)
            t_kvl = P.toks(4)
            sqr = SRing(2, [GT], BF16, with_sem=False)
            rstd_a = A.alloc([GT], F32)
            t_rstd_a = P.tok()
            kvnr = SRing(2, [4, GT], BF16)
            zr = A.alloc([GT], BF16, 64)
            t_zr = P.tok()
            cos_a = A.alloc([GT], F32, 64)
            sin_a = A.alloc([GT], F32, 64)
            t_cs_a = P.tok()
            rtmp = rope_tmp(GT)
            rt1 = A.alloc([GT], F32, 64)
            rt2 = A.alloc([GT], F32, 64)
            t_rt = P.tok()

            xbs_cur = fe_load(lambda i: x_all[i * 128:(i + 1) * 128, :], GA_, xring, junk, t_junk, ssb, t_ss, C_GMIXPRE)
            for g in range(NG):
                fe_transpose(xbs_cur, GA_, C_GMIXPRE, hTa, t_hTa, 0)
                rope_tables(pos_all[:, g * GT:(g + 1) * GT], GT, cos_a, sin_a, t_cs_a, rtmp)
                if g + 1 < NG:
                    xbs_cur = fe_load(lambda i, g=g: x_all[((g + 1) * GA_ + i) * 128:((g + 1) * GA_ + i + 1) * 128, :], GA_, xring, junk, t_junk, ssb, t_ss, C_GMIXPRE)
                pkv = []
                for m in range(4):
                    pa, pt = ps_next()
                    for kt in range(FT):
                        P.op("pe", lambda e, pa=pa, kt=kt, m=m: e.matmul(pa[:, 0:GT], lhsT=wkvin[:, kt, m * 128:(m + 1) * 128], rhs=hTa[:, kt, :], start=(kt == 0), stop=(kt == FT - 1)),
                             reads=[t_wres, t_hTa[kt]], writes=[pt])
                    pkv.append((pa, pt))
                pr, ptr = ps_next()
                for kt in range(FT):
                    P.op("pe", lambda e, kt=kt, pr=pr: e.matmul(pr[0:64, 0:GT], lhsT=wkvin[:, kt, 512:576], rhs=hTa[:, kt, :], start=(kt == 0), stop=(kt == FT - 1)),
                         reads=[t_wres, t_hTa[kt]], writes=[ptr])
                acc = (psum[7], ptok[7])
                for m in range(4):
                    pa, pt = pkv[m]
                    P.op("act", lambda e, pa=pa, m=m: e.activation(out=kvl[:, m, :], in_=pa[:, 0:GT], func=AF.Copy), reads=[pt], writes=[t_kvl[m]])
                    sq, t_sq, _ = sqr.next()
                    P.op("act", lambda e, pa=pa, sq=sq: e.activation(out=sq, in_=pa[:, 0:GT], func=AF.Square), reads=[pt], writes=[t_sq])
                    sumsq_accum(sq, t_sq, acc, m == 0, m == 3, GT)
                rstd_from_psum([acc], 512, rstd_a, t_rstd_a, [(0, GT)])
                kvn, t_kvn, s_kvn = kvnr.next()
                for m in range(4):
                    P.op("dve", lambda e, m=m, kvn=kvn: e.scalar_tensor_tensor(out=kvn[:, m, :], in0=kvl[:, m, :], scalar=cc(C_KVN + m), in1=rstd_a, op0=OP.mult, op1=OP.mult),
                         reads=[t_kvl[m], t_rstd_a, t_cst], writes=[t_kvn])
                P.op("sp", lambda e, kvn=kvn, g=g: e.dma_start(out=KVN_scr[:, :, g * GT:(g + 1) * GT].rearrange("m p t -> p m t"), in_=kvn),
                     reads=[t_kvn], writes=[T_KVN[g]], dma_sem=s_kvn)
                P.op("act", lambda e, pr=pr: e.activation(out=zr, in_=pr[0:64, 0:GT], func=AF.Copy), reads=[ptr], writes=[t_zr])
                prot, ptrot = ps_next()
                P.op("pe", lambda e, prot=prot: e.matmul(prot[0:64, 0:GT], lhsT=rot_bf[0:64, 0:64], rhs=zr, start=True, stop=True), reads=[t_zr, t_mats], writes=[ptrot])
                P.op("dve", lambda e: e.tensor_tensor(out=rt1, in0=zr, in1=cos_a, op=OP.mult), reads=[t_zr, t_cs_a], writes=[t_rt])
                P.op("dve", lambda e, prot=prot: e.tensor_tensor(out=rt2, in0=prot[0:64, 0:GT], in1=sin_a, op=OP.mult), reads=[ptrot, t_cs_a], writes=[t_rt])
                P.op("dve", lambda e, g=g: e.tensor_tensor(out=kr[:, g * GT:(g + 1) * GT], in0=rt1, in1=rt2, op=OP.add), reads=[t_rt], writes=[t_kr[g]])
            P.barrier(dummy[:, 0:1])

            A.reset(A_BASE1)
            wk_sb = A.alloc([4, NH * 128], BF16)
            wv_sb = A.alloc([4, NH * 128], BF16)
            t_wres2 = P.tok()
            P.op("pool", lambda e: e.dma_start(out=wk_sb, in_=w_k), writes=[t_wres2], dma_sem=sem())
            P.op("pool", lambda e: e.dma_start(out=wv_sb, in_=w_v), writes=[t_wres2], dma_sem=sem())
            kvin = SRing(3, [4, GT], BF16)
            kst = SRing(2, [4, GT], BF16)
            vst = SRing(2, [NH, 128], BF16)
            for g in range(NG):
                kvn, t_kvn, s_kvn = kvin.next()
                P.op("sp", lambda e, kvn=kvn, g=g: e.dma_start(out=kvn, in_=KVN_scr[:, :, g * GT:(g + 1) * GT].rearrange("m p t -> p m t")),
                     reads=[T_KVN[g]], writes=[t_kvn], dma_sem=s_kvn)
                for hg in range(4):
                    ks, t_ks, s_ks = kst.next()
                    for hh in range(4):
                        h = hg * 4 + hh
                        pa, pt = ps_next(8)
                        for kt in range(4):
                            P.op("pe", lambda e, pa=pa, kt=kt, h=h, kvn=kvn: e.matmul(pa[:, 0:GT], lhsT=wk_sb[:, kt, h * 128:(h + 1) * 128], rhs=kvn[:, kt, :], start=(kt == 0), stop=(kt == 3)),
                                 reads=[t_wres2, t_kvn], writes=[pt])
                        plain_copy(evac_eng(), ks[:, hh, :], pa[:, 0:GT], [pt], [t_ks])
                    P.op("sp", lambda e, ks=ks, hg=hg, g=g: e.dma_start(out=K_scr[hg * 4:(hg + 1) * 4, :, g * GT:(g + 1) * GT].rearrange("h p t -> p h t"), in_=ks),
                         reads=[t_ks], writes=[T_K[hg][g]], dma_sem=s_ks)
                for j in range(GA_):
                    vs, t_vs, s_vs = vst.next()
                    for hg in range(4):
                        pa, pt = ps_next(8)
                        for kt in range(4):
                            P.op("pe", lambda e, pa=pa, kt=kt, j=j, hg=hg, kvn=kvn: e.matmul(pa[:, 0:512], lhsT=kvn[:, kt, j * 128:(j + 1) * 128], rhs=wv_sb[:, kt, hg * 512:(hg + 1) * 512], start=(kt == 0), stop=(kt == 3)),
                                 reads=[t_wres2, t_kvn], writes=[pt])
                        plain_copy(evac_eng(), vs[:, hg * 4:(hg + 1) * 4, :], pa[:, 0:512].rearrange("p (h d) -> p h d", d=128), [pt], [t_vs])
                    ktg = g * GA_ + j
                    P.op("sp", lambda e, vs=vs, ktg=ktg: e.dma_start(out=V_scr[:, ktg, :, :].rearrange("h p d -> p h d"), in_=vs),
                         reads=[t_vs], writes=[T_V[ktg]], dma_sem=s_vs)
            A.n = ARENA_BYTES

            P.barrier(dummy[:, 0:1])
            ckpt(0)

            A.reset(A_BASE1)
            qn = A.alloc([8, NT], BF16)
            t_qn = P.toks(8)
            A_BASE2 = A.off
            hT = A.alloc([FT, NT], BF16)
            t_hT = P.toks(FT)
            hTh = A.alloc([FT, 128], BF16)
            t_hTh = P.toks(FT)
            B_MARK = A.off
            xring = SRing(4, [D], BF16)
            junk = A.alloc([D], BF16)
            t_junk = P.tok()
            ssb = A.alloc([8], F32)
            t_ss = P.toks(8)
            for gg in range(2):
                front_end(lambda i, gg=gg: x_own[(gg * 4 + i) * 128:(gg * 4 + i + 1) * 128, :], 4, xring, junk, t_junk, ssb, t_ss, C_GMIXPRE, hT, t_hT, gg * 512)
            front_end(lambda i: x_halo, 1, xring, junk, t_junk, ssb, t_ss, C_GMIXPRE, hTh, t_hTh, 0)
            P.barrier(dummy[:, 0:1], skip=wtok)
            ckpt(1)
            A.reset(B_MARK)
            rh = lambda kt: (hT[:, kt, :], t_hT[kt])
            rhh = lambda kt: (hTh[:, kt, :], t_hTh[kt])
            CH_OWN = [(rh, 0, 512), (rh, 512, 512)]

            sgst = SRing(3, [NT], BF16)
            for m in range(64):
                pss = proj(w_gate, m, FT, CH_OWN)
                st, t_st, s_st = sgst.next()
                for (pa, pt), (cs, cn) in zip(pss, CH2):
                    P.op("act", lambda e, pa=pa, st=st, cs=cs, cn=cn: e.activation(out=st[:, cs:cs + cn], in_=pa[:, 0:cn], func=AF.Sigmoid), reads=[pt], writes=[t_st])
                dst, tdst = (GA_scr[m], T_GA[m]) if m < 32 else (GB_scr[m - 32], T_GB[m - 32])
                P.op("sp", lambda e, dst=dst, st=st: e.dma_start(out=dst, in_=st), reads=[t_st], writes=[tdst], dma_sem=s_st)

            ckpt('g')
            sqr = SRing(2, [512], BF16, with_sem=False)
            rstd_q = A.alloc([NT], F32)
            t_rstd_q = P.tok()
            accs = [(psum[6], ptok[6]), (psum[7], ptok[7])]
            for mi in range(8):
                pss = proj(w_in_main, 16 + mi, FT, CH_OWN)
                for ci, ((pa, pt), (cs, cn)) in enumerate(zip(pss, CH2)):
                    P.op("dve", lambda e, pa=pa, mi=mi, cs=cs, cn=cn: e.tensor_copy(out=qn[:, mi, cs:cs + cn], in_=pa[:, 0:cn]), reads=[pt], writes=[t_qn[mi]])
                    sq, t_sq, _ = sqr.next()
                    P.op("act", lambda e, pa=pa, sq=sq, cn=cn: e.activation(out=sq, in_=pa[:, 0:cn], func=AF.Square), reads=[pt], writes=[t_sq])
                    sumsq_accum(sq, t_sq, accs[ci], mi == 0, mi == 7, cn)
            rstd_from_psum(accs, 1024, rstd_q, t_rstd_q, CH2)
            for mi in range(8):
                P.op("dve", lambda e, mi=mi: e.scalar_tensor_tensor(out=qn[:, mi, :], in0=qn[:, mi, :], scalar=cc(C_QN + mi), in1=rstd_q, op0=OP.mult, op1=OP.mult),
                     reads=[t_qn[mi], t_rstd_q, t_cst], writes=[t_qn[mi]])

            ckpt('q')
            uext = A.alloc([NBLK, 144], F32)
            t_u = P.tok()
            sA = A.alloc([NBLK * 144], F32)
            sB = A.alloc([NBLK * 144], F32)
            t_s = P.tok()
            P.op("dve", lambda e: e.memset(sA, 0.0), writes=[t_s])
            P.op("dve", lambda e: e.memset(sB, 0.0), writes=[t_s])
            pooled = SRing(1, [4, NT], BF16, with_sem=False)
            tmp16 = A.alloc([16], F32)
            pmst = SRing(2, [NT], BF16)
            uflat = uext.rearrange("p b t -> p (b t)")
            for g in range(4):
                w = 2 ** (g + 1)
                pl, t_pl, _ = pooled.next()
                for mm in range(4):
                    m = g * 4 + mm
                    pss = proj(w_in_main, m, FT, CH_OWN + [(rhh, 0, 128)])
                    for ci in range(2):
                        pa, pt = pss[ci]
                        plain_copy(evac_eng(), uext[:, 4 * ci:4 * ci + 4, 16:144], pa[:, 0:512].rearrange("p (b t) -> p b t", t=128), [pt], [t_u])
                    pa, pt = pss[2]
                    plain_copy(evac_eng(), uext[:, :, 0:16], pa[:, 0:128].rearrange("p (b t) -> p b t", t=16), [pt], [t_u])
                    cur = uflat
                    bufs = [sA, sB]
                    for k in range(g + 1):
                        sh = 2 ** k
                        nxt = bufs[k % 2]
                        P.op("dve", lambda e, cur=cur, nxt=nxt, sh=sh: e.tensor_tensor(out=nxt[:, sh:], in0=cur[:, sh:], in1=cur[:, 0:NBLK * 144 - sh], op=OP.add), reads=[t_u, t_s], writes=[t_s])
                        cur = nxt
                    s3 = cur.rearrange("p (b t) -> p b t", t=144)
                    P.op("dve", lambda e, s3=s3, pl=pl, mm=mm, w=w: e.scalar_tensor_tensor(out=pl[:, mm, :].rearrange("p (b t) -> p b t", t=128), in0=s3[:, :, 16:144], scalar=1.0 / w, in1=uext[:, :, 16:144], op0=OP.mult, op1=OP.subtract),
                         reads=[t_s, t_u], writes=[t_pl])
                    P.op("dve", lambda e, cur=cur, g=g: e.tensor_tensor(out=tmp16, in0=cur[:, 16:32], in1=cst[:, C_PCORR + g * 16:C_PCORR + (g + 1) * 16], op=OP.mult), reads=[t_s, t_cst], writes=[t_s])
                    P.op("dve", lambda e, pl=pl, mm=mm: e.tensor_tensor(out=pl[:, mm, 0:16], in0=pl[:, mm, 0:16], in1=tmp16, op=OP.add), reads=[t_s, t_pl], writes=[t_pl])
                rp = lambda kt, pl=pl, t_pl=t_pl: (pl[:, kt, :], t_pl)
                for mm in range(4):
                    m = g * 4 + mm
                    pss = proj(w_pool, m, 4, [(rp, 0, 512), (rp, 512, 512)])
                    st, t_st, s_st = pmst.next()
                    for (pa, pt), (cs, cn) in zip(pss, CH2):
                        scale_copy(evac_eng(), st[:, cs:cs + cn], pa[:, 0:cn], cc(C_PSC + m), [pt, t_cst], [t_st])
                    P.op("sp", lambda e, st=st, m=m: e.dma_start(out=PM_scr[m], in_=st), reads=[t_st], writes=[T_PM[m]], dma_sem=s_st)

            P.barrier(dummy[:, 0:1], skip=wtok)
            ckpt(2)

            A.reset(A_BASE2)
            attnT = A.alloc([NH, NT], BF16)
            t_attn = P.toks(NH)
            C_MARK = A.off
            cos_o = A.alloc([NT], F32, 64)
            sin_o = A.alloc([NT], F32, 64)
            t_cs_o = P.tok()
            rtmp = rope_tmp(NT)
            rope_tables(pos_own, NT, cos_o, sin_o, t_cs_o, rtmp)
            P.barrier(dummy[:, 0:1], skip=wtok)
            ckpt(3)
            A.reset(C_MARK + 2 * 4096)
            NKC = 4
            KPC = 64 // NKC
            kring = SRing(5, [S // NKC], BF16)
            vring = SRing(5, [KPC, 128], BF16)
            qnope = SRing(2, [NT], BF16, with_sem=False)
            qrope = SRing(2, [NT], BF16, 64, with_sem=False)
            qz = A.alloc([NT], BF16, 64)
            t_qz = P.tok()
            q1 = A.alloc([512], F32, 64)
            q2 = A.alloc([512], F32, 64)
            t_q12 = P.tok()
            pring = SRing(4, [NT], BF16, with_sem=False)
            daccs = [(A.alloc([NT], F32), P.tok())] * 2
            osb = A.alloc([NT], F32)
            t_osb = P.tok()
            rec = A.alloc([NT], F32)
            t_rec = P.tok()
            maskm = A.alloc([8, 128], BF16)
            t_maskm = P.tok()
            P.op("pool", lambda e: e.dma_start(out=maskm, in_=maskm_d), writes=[t_maskm], dma_sem=sem())
            NRR = 6
            LOOK = 3
            qbufs = {}
            cbufs = {}

            def emit_q(h):
                wq3, t_wq = load_w(w_q[h], 8, 192)
                qn_, t_qn_, _ = qnope.next()
                qr_, t_qr_, _ = qrope.next()
                for ci, (cs, cn) in enumerate(CH2):
                    pa, pt = ps_next(NRR)
                    for kt in range(8):
                        P.op("pe", lambda e, pa=pa, kt=kt, cs=cs, cn=cn, wq3=wq3: e.matmul(pa[:, 0:cn], lhsT=wq3[:, kt, 0:128], rhs=qn[:, kt, cs:cs + cn], start=(kt == 0), stop=(kt == 7)),
                             reads=[t_wq, t_qn[kt]], writes=[pt])
                    plain_copy("dve", qn_[:, cs:cs + cn], pa[:, 0:cn], [pt], [t_qn_])
                    pb, ptb = ps_next(NRR)
                    for kt in range(8):
                        P.op("pe", lambda e, pb=pb, kt=kt, cs=cs, cn=cn, wq3=wq3: e.matmul(pb[0:64, 0:cn], lhsT=wq3[:, kt, 128:192], rhs=qn[:, kt, cs:cs + cn], start=(kt == 0), stop=(kt == 7)),
                             reads=[t_wq, t_qn[kt]], writes=[ptb])
                    P.op("dve", lambda e, pb=pb, cs=cs, cn=cn: e.tensor_copy(out=qz[:, cs:cs + cn], in_=pb[0:64, 0:cn]), reads=[ptb], writes=[t_qz])
                    prot, ptrot = ps_next(NRR)
                    P.op("pe", lambda e, prot=prot, cs=cs, cn=cn: e.matmul(prot[0:64, 0:cn], lhsT=rot_bf[0:64, 0:64], rhs=qz[:, cs:cs + cn], start=True, stop=True), reads=[t_qz, t_mats], writes=[ptrot])
                    P.op("dve", lambda e, cs=cs, cn=cn: e.tensor_tensor(out=q1[:, 0:cn], in0=qz[:, cs:cs + cn], in1=cos_o[:, cs:cs + cn], op=OP.mult), reads=[t_qz, t_cs_o], writes=[t_q12])
                    P.op("dve", lambda e, prot=prot, cs=cs, cn=cn: e.tensor_tensor(out=q2[:, 0:cn], in0=prot[0:64, 0:cn], in1=sin_o[:, cs:cs + cn], op=OP.mult), reads=[ptrot, t_cs_o, t_q12], writes=[t_q12])
                    P.op("dve", lambda e, qr_=qr_, cs=cs, cn=cn: e.tensor_tensor(out=qr_[:, cs:cs + cn], in0=q1[:, 0:cn], in1=q2[:, 0:cn], op=OP.add), reads=[t_q12], writes=[t_qr_])
                qbufs[h] = (qn_, t_qn_, qr_, t_qr_)

            def load_chunk(n):
                h, kc = n // NKC, n % NKC
                kk_, t_kk, s_kk = kring.next()
                vv_, t_vv, s_vv = vring.next()
                kt0 = kc * KPC
                g0 = kc * (NG // NKC)
                P.op("sp", lambda e: e.dma_start(out=kk_, in_=K_scr[h, :, kc * (S // NKC):(kc + 1) * (S // NKC)]),
                     reads=[T_K[h // 4][g] for g in range(g0, g0 + NG // NKC)], writes=[t_kk], dma_sem=s_kk)
                P.op("sp", lambda e: e.dma_start(out=vv_, in_=V_scr[h, kt0:kt0 + KPC, :, :].rearrange("k p d -> p k d")),
                     reads=[T_V[k] for k in range(kt0, kt0 + KPC)], writes=[t_vv], dma_sem=s_vv)
                cbufs[n] = (kk_, t_kk, vv_, t_vv)

            def chunks_of(kb):
                c0 = (kb // 8) * 128
                if c0 < 512:
                    return c0, [(c0, 512 - c0), (512, 512)]
                return c0, [(c0, NT - c0)]

            def phase1(h, kb):
                kk_, t_kk, vv_, t_vv = cbufs[h * NKC + kb // KPC]
                qn_, t_qn_, qr_, t_qr_ = qbufs[h]
                kl = kb % KPC
                cp = kb % 8
                c0, chs = chunks_of(kb)
                pt_, t_pt_, _ = pring.next()
                for (cs, cn) in chs:
                    pa, pt = ps_next(NRR)
                    P.op("pe", lambda e, pa=pa, cs=cs, cn=cn: e.matmul(pa[:, 0:cn], lhsT=kk_[:, kl * 128:(kl + 1) * 128], rhs=qn_[:, cs:cs + cn], start=True, stop=False),
                         reads=[t_kk, t_qn_], writes=[pt])
                    first = (cs == c0)
                    P.op("pe", lambda e, pa=pa, cs=cs, cn=cn, first=first: e.matmul(pa[:, 0:cn], lhsT=kr[:, kb * 128:(kb + 1) * 128], rhs=qr_[:, cs:cs + cn], start=False, stop=(not first)),
                         reads=[t_kr[kb // GA_], t_qr_], writes=[pt])
                    if first:
                        P.op("pe", lambda e, pa=pa: e.matmul(pa[:, 0:128], lhsT=ident_bf, rhs=maskm[:, cp, :], start=False, stop=True), reads=[t_maskm, t_mats], writes=[pt])
                    P.op("act", lambda e, pa=pa, cs=cs, cn=cn: e.activation(out=pt_[:, cs:cs + cn], in_=pa[:, 0:cn], func=AF.Exp, scale=SCALE), reads=[pt], writes=[t_pt_])
                return (pt_, t_pt_)

            first_o = {}

            def phase2(h, kb, st):
                pt_, t_pt_ = st
                kk_, t_kk, vv_, t_vv = cbufs[h * NKC + kb // KPC]
                kl = kb % KPC
                c0, chs = chunks_of(kb)
                ob = 6
                fo = first_o.setdefault(h, [True, True])
                dacc, t_dacc = daccs[h % 2]
                for (cs, cn) in chs:
                    b = cs // 512
                    assert (cs + cn - 1) // 512 == b
                    oa, ot = psum[ob + b], ptok[ob + b]
                    last = (kb == 63) if b == 1 else (kb == 31)
                    P.op("pe", lambda e, oa=oa, cs=cs, cn=cn, b=b, stt=fo[b], last=last: e.matmul(oa[:, cs - b * 512:cs - b * 512 + cn], lhsT=vv_[:, kl, :], rhs=pt_[:, cs:cs + cn], start=stt, stop=last),
                         reads=[t_vv, t_pt_], writes=[ot])
                    fo[b] = False
                if kb == 0:
                    P.op("dve", lambda e: e.tensor_copy(out=dacc, in_=pt_), reads=[t_pt_], writes=[t_dacc])
                else:
                    P.op("dve", lambda e: e.tensor_tensor(out=dacc[:, c0:], in0=dacc[:, c0:], in1=pt_[:, c0:], op=OP.add), reads=[t_pt_, t_dacc], writes=[t_dacc])

            def epilogue(h):
                ob = 6
                dacc, t_dacc = daccs[h % 2]
                for b, (cs, cn) in enumerate(CH2):
                    oa, ot = psum[ob + b], ptok[ob + b]
                    P.op("dve", lambda e, oa=oa, cs=cs, cn=cn: e.tensor_copy(out=osb[:, cs:cs + cn], in_=oa[:, 0:cn]), reads=[ot], writes=[t_osb])
                for b, (cs, cn) in enumerate(CH2):
                    pa, pt = ps_next(NRR)
                    P.op("pe", lambda e, pa=pa, cs=cs, cn=cn: e.matmul(pa[:, 0:cn], lhsT=mats_f[:, 1, :], rhs=dacc[:, cs:cs + cn], start=True, stop=True), reads=[t_dacc, t_mats], writes=[pt])
                    P.op("dve", lambda e, pa=pa, cs=cs, cn=cn: e.reciprocal(out=rec[:, cs:cs + cn], in_=pa[:, 0:cn]), reads=[pt], writes=[t_rec])
                    P.op("dve", lambda e, cs=cs, cn=cn: e.tensor_tensor(out=attnT[:, h, cs:cs + cn], in0=osb[:, cs:cs + cn], in1=rec[:, cs:cs + cn], op=OP.mult), reads=[t_osb, t_rec], writes=[t_attn[h]])

            emit_q(0)
            load_chunk(0)
            load_chunk(1)
            load_chunk(2)
            pend = []
            for h in range(NH):
                for kb in range(64):
                    if kb % KPC == 0:
                        n = h * NKC + kb // KPC + 3
                        if n < NH * NKC:
                            load_chunk(n)
                    if kb == 40 and h + 1 < NH:
                        emit_q(h + 1)
                    st = phase1(h, kb)
                    pend.append((h, kb, st))
                    if len(pend) > LOOK:
                        p0 = pend.pop(0)
                        phase2(*p0)
                        if p0[1] == 63:
                            epilogue(p0[0])
            for p0 in pend:
                phase2(*p0)
                if p0[1] == 63:
                    epilogue(p0[0])

            P.barrier(dummy[:, 0:1], skip=wtok)
            ckpt(4)

            A.reset(0)
            pmT = A.alloc([16, NT], BF16)
            t_pmT = P.toks(16)
            assert A.off <= A_BASE2
            A.reset(A_BASE2 + NH * NT * 2)
            mergedT = A.alloc([FT, NT], BF16)
            t_mg = P.toks(FT)
            garing = SRing(2, [NT], BF16)
            gbring = SRing(2, [NT], BF16)
            t1r = SRing(2, [512], F32, with_sem=False)
            t2r = SRing(2, [512], F32, with_sem=False)
            for m in range(16):
                P.op("sp", lambda e, m=m: e.dma_start(out=pmT[:, m, :], in_=PM_scr[m]), reads=[T_PM[m]], writes=[t_pmT[m]], dma_sem=sem())
            rpm = lambda kt: (pmT[:, kt, :], t_pmT[kt])
            rat = lambda kt: (attnT[:, kt, :], t_attn[kt])
            for m in range(FT):
                ga_, t_ga_, s_ga_ = garing.next()
                gb_, t_gb_, s_gb_ = gbring.next()
                P.op("sp", lambda e, ga_=ga_, m=m: e.dma_start(out=ga_, in_=GA_scr[m]), reads=[T_GA[m]], writes=[t_ga_], dma_sem=s_ga_)
                P.op("sp", lambda e, gb_=gb_, m=m: e.dma_start(out=gb_, in_=GB_scr[m]), reads=[T_GB[m]], writes=[t_gb_], dma_sem=s_gb_)
                psa = proj(w_up_pool, m, 16, [(rpm, 0, 512), (rpm, 512, 512)], nrr=8)
                psb = proj(w_up_mla, m, 16, [(rat, 0, 512), (rat, 512, 512)], nrr=8)
                for ci, (cs, cn) in enumerate(CH2):
                    t1, t_t1, _ = t1r.next()
                    t2, t_t2, _ = t2r.next()
                    P.op("dve", lambda e, t1=t1, pa=psa[ci][0], ga_=ga_, cs=cs, cn=cn: e.tensor_tensor(out=t1, in0=pa[:, 0:cn], in1=ga_[:, cs:cs + cn], op=OP.mult), reads=[psa[ci][1], t_ga_], writes=[t_t1])
                    P.op("dve", lambda e, t2=t2, pb=psb[ci][0], gb_=gb_, cs=cs, cn=cn: e.tensor_tensor(out=t2, in0=pb[:, 0:cn], in1=gb_[:, cs:cs + cn], op=OP.mult), reads=[psb[ci][1], t_gb_], writes=[t_t2])
                    P.op("dve", lambda e, t1=t1, t2=t2, m=m, cs=cs, cn=cn: e.tensor_tensor(out=mergedT[:, m, cs:cs + cn], in0=t1, in1=t2, op=OP.add), reads=[t_t1, t_t2], writes=[t_mg[m]])
            P.barrier(dummy[:, 0:1], skip=wtok)
            ckpt(5)

            def out_proj_stage(Wt, KT, rhs_fn, scr, T_scr, base_off, post=None, kchunk=32):
                A.reset(base_off)
                stg = SRing(2, [NT], F32)
                sq_r = SRing(2, [512], BF16, with_sem=False)
                rstd = A.alloc([NT], F32)
                t_rstd = P.tok()
                accs_ = [(psum[6], ptok[6]), (psum[7], ptok[7])]
                for m in range(FT):
                    pss = proj(Wt, m, KT, [(rhs_fn, 0, 512), (rhs_fn, 512, 512)], kchunk=kchunk)
                    st, t_st, s_st = stg.next()
                    for ci, ((pa, pt), (cs, cn)) in enumerate(zip(pss, CH2)):
                        if post is not None:
                            post(m, ci, pa, pt, st[:, cs:cs + cn], t_st)
                            src_ap, src_t = st[:, cs:cs + cn], t_st
                        else:
                            P.op("dve", lambda e, pa=pa, st=st, cs=cs, cn=cn: e.tensor_copy(out=st[:, cs:cs + cn], in_=pa[:, 0:cn]), reads=[pt], writes=[t_st])
                            src_ap, src_t = pa[:, 0:cn], pt
                        sq, t_sq, _ = sq_r.next()
                        P.op("act", lambda e, src_ap=src_ap, sq=sq: e.activation(out=sq, in_=src_ap, func=AF.Square), reads=[src_t], writes=[t_sq])
                        sumsq_accum(sq, t_sq, accs_[ci], m == 0, m == FT - 1, cn)
                    P.op("sp", lambda e, st=st, m=m: e.dma_start(out=scr[m], in_=st), reads=[t_st], writes=[T_scr[m]], dma_sem=s_st)
                rstd_from_psum(accs_, D, rstd, t_rstd, CH2)
                return rstd, t_rstd

            def resid_stage(src_scr, T_src, rstd, t_rstd, g_post_c, resid_kind, resid_scr, T_resid, dst_scr, T_dst, g_pre_c, hnext, t_hnext, final=False):
                srcr = SRing(3, [NT], F32)
                resr = SRing(3, [NT], F32) if resid_kind == "scr" else SRing(3, [NBLK, 128], F32)
                xnr = SRing(3, [NT], F32)
                tmpr = SRing(3, [NT], F32, with_sem=False)
                sq_r = SRing(2, [512], BF16, with_sem=False)
                outr = SRing(2, [NBLK, 128], F32) if final else None
                accs_ = [(psum[6], ptok[6]), (psum[7], ptok[7])]
                loaded = {}

                def issue_loads(m):
                    sr, t_sr, s_sr = srcr.next()
                    P.op("sp", lambda e: e.dma_start(out=sr, in_=src_scr[m]), reads=(T_src[m] if isinstance(T_src[m], list) else [T_src[m]]), writes=[t_sr], dma_sem=s_sr)
                    rr, t_rr, s_rr = resr.next()
                    if resid_kind == "scr":
                        P.op("sp", lambda e: e.dma_start(out=rr, in_=resid_scr[m]), reads=[T_resid[m]], writes=[t_rr], dma_sem=s_rr)
                    else:
                        P.op("sp", lambda e: e.dma_start(out=rr, in_=x_own[:, m * 128:(m + 1) * 128].rearrange("(b p) f -> p b f", p=128)), writes=[t_rr], dma_sem=s_rr)
                    loaded[m] = (sr, t_sr, rr, t_rr)

                issue_loads(0)
                issue_loads(1)
                for m in range(FT):
                    if m + 2 < FT:
                        issue_loads(m + 2)
                    sr, t_sr, rr, t_rr = loaded.pop(m)
                    tm, t_tm, _ = tmpr.next()
                    xn, t_xn, s_xn = xnr.next()
                    P.op("dve", lambda e, tm=tm, sr=sr, m=m: e.scalar_tensor_tensor(out=tm, in0=sr, scalar=cc(g_post_c + m), in1=rstd, op0=OP.mult, op1=OP.mult), reads=[t_sr, t_rstd, t_cst], writes=[t_tm])
                    if resid_kind == "scr":
                        P.op("dve", lambda e, xn=xn, tm=tm, rr=rr: e.tensor_tensor(out=xn, in0=tm, in1=rr, op=OP.add), reads=[t_tm, t_rr], writes=[t_xn])
                    else:
                        for ci, (cs, cn) in enumerate(CH2):
                            pa, pt = ps_next()
                            for bb in range(4):
                                b = ci * 4 + bb
                                P.op("pe", lambda e, pa=pa, rr=rr, b=b, bb=bb: e.transpose(out=pa[:, bb * 128:(bb + 1) * 128], in_=rr[:, b, :], identity=ident_f), reads=[t_rr, t_mats], writes=[pt])
                            P.op("dve", lambda e, xn=xn, tm=tm, pa=pa, cs=cs, cn=cn: e.tensor_tensor(out=xn[:, cs:cs + cn], in0=tm[:, cs:cs + cn], in1=pa[:, 0:cn], op=OP.add), reads=[t_tm, pt], writes=[t_xn])
                    if not final:
                        P.op("sp", lambda e, xn=xn, m=m: e.dma_start(out=dst_scr[m], in_=xn), reads=[t_xn], writes=[T_dst[m]], dma_sem=s_xn)
                        P.op("act", lambda e, xn=xn, m=m: e.activation(out=hnext[:, m, :], in_=xn, func=AF.Copy, scale=cc(g_pre_c + m)), reads=[t_xn, t_cst], writes=[t_hnext[m]])
                        for ci, (cs, cn) in enumerate(CH2):
                            sq, t_sq, _ = sq_r.next()
                            P.op("act", lambda e, xn=xn, sq=sq, cs=cs, cn=cn: e.activation(out=sq, in_=xn[:, cs:cs + cn], func=AF.Square), reads=[t_xn], writes=[t_sq])
                            sumsq_accum(sq, t_sq, accs_[ci], m == 0, m == FT - 1, cn)
                    else:
                        ot_, t_ot, s_ot = outr.next()
                        for ci, (cs, cn) in enumerate(CH2):
                            pa, pt = ps_next()
                            for bb in range(4):
                                b = ci * 4 + bb
                                P.op("pe", lambda e, pa=pa, xn=xn, b=b, bb=bb: e.transpose(out=pa[:, bb * 128:(bb + 1) * 128], in_=xn[:, b * 128:(b + 1) * 128], identity=ident_f), reads=[t_xn, t_mats], writes=[pt])
                            plain_copy(evac_eng(), ot_[:, ci * 4:ci * 4 + 4, :], pa[:, 0:512].rearrange("p (b f) -> p b f", f=128), [pt], [t_ot])
                        P.op("sp", lambda e, ot_=ot_, m=m: e.dma_start(out=out_d[:, m * 128:(m + 1) * 128].rearrange("(b p) f -> p b f", p=128), in_=ot_), reads=[t_ot], writes=[T_OUT[m]], dma_sem=s_ot)
                if not final:
                    rstd2 = A.alloc([NT], F32)
                    t_rstd2 = P.tok()
                    rstd_from_psum(accs_, D, rstd2, t_rstd2, CH2)
                    for m in range(FT):
                        P.op("dve", lambda e, m=m: e.tensor_tensor(out=hnext[:, m, :], in0=hnext[:, m, :], in1=rstd2, op=OP.mult), reads=[t_hnext[m], t_rstd2], writes=[t_hnext[m]])

            MG_END = A_BASE2 + NH * NT * 2 + FT * NT * 2
            rmg = lambda kt: (mergedT[:, kt, :], t_mg[kt])
            rstd1, t_rstd1 = out_proj_stage(w_out, FT, rmg, MIX_scr, T_MIX, MG_END)
            P.barrier(dummy[:, 0:1], skip=wtok)
            ckpt(6)

            A.reset(A_BASE1)
            rstd_keep = A.alloc([NT], F32)
            t_rk = P.tok()
            P.op("dve", lambda e: e.tensor_copy(out=rstd_keep, in_=rstd1), reads=[t_rstd1], writes=[t_rk])
            P.barrier(dummy[:, 0:1], skip=wtok)
            ckpt(7)
            h2 = A.alloc([FT, NT], BF16)
            t_h2 = P.toks(FT)
            H_END = A.off
            resid_stage(MIX_scr, T_MIX, rstd_keep, t_rk, C_GMIXPOST, "x", None, None, X1_scr, T_X1, C_GFFNPRE, h2, t_h2)
            P.barrier(dummy[:, 0:1], skip=wtok)
            ckpt(8)

            A.reset(H_END)
            sgr = SRing(2, [512], F32, with_sem=False)
            actst = SRing(3, [NT], BF16)
            rh2 = lambda kt: (h2[:, kt, :], t_h2[kt])
            for f in range(FFT):
                psg = proj(w_fg, f, FT, [(rh2, 0, 512), (rh2, 512, 512)], nrr=8)
                psu = proj(w_fu, f, FT, [(rh2, 0, 512), (rh2, 512, 512)], nrr=8)
                st, t_st, s_st = actst.next()
                for ci, (cs, cn) in enumerate(CH2):
                    sg, t_sg, _ = sgr.next()
                    P.op("act", lambda e, sg=sg, pa=psg[ci][0]: e.activation(out=sg, in_=pa[:, 0:512], func=AF.Silu), reads=[psg[ci][1]], writes=[t_sg])
                    P.op("dve", lambda e, sg=sg, pu=psu[ci][0], st=st, cs=cs, cn=cn: e.tensor_tensor(out=st[:, cs:cs + cn], in0=pu[:, 0:cn], in1=sg, op=OP.mult), reads=[psu[ci][1], t_sg], writes=[t_st])
                P.op("sp", lambda e, st=st, f=f: e.dma_start(out=ACT_scr[f], in_=st), reads=[t_st], writes=[T_ACT[f]], dma_sem=s_st)
            P.barrier(dummy[:, 0:1], skip=wtok)
            ckpt(9)

            A.reset(A_BASE1 + NT * 4)
            actT = A.alloc([FFT, 512], BF16)
            t_actT = P.toks(FFT)
            stg = SRing(2, [512], F32)
            sq_r = SRing(2, [512], BF16, with_sem=False)
            rstd_f = A.alloc([NT], F32)
            t_rstd_f = P.tok()
            accs = [(psum[6], ptok[6]), (psum[7], ptok[7])]
            for half in range(2):
                for f0 in range(0, FFT, 8):
                    f1 = min(FFT, f0 + 8)
                    s_ = sem()
                    P.op("sp", lambda e, f0=f0, f1=f1, half=half: e.dma_start(out=actT[:, f0:f1, :], in_=ACT_scr[f0:f1, :, half * 512:(half + 1) * 512].rearrange("f p t -> p f t")),
                         reads=[T_ACT[f] for f in range(f0, f1)], writes=[t_actT[f] for f in range(f0, f1)], dma_sem=s_)
                ract = lambda kt: (actT[:, kt, :], t_actT[kt])
                for m in range(FT):
                    pss = proj(w_fd, m, FFT, [(ract, 0, 512)])
                    pa, pt = pss[0]
                    st, t_st, s_st = stg.next()
                    P.op("dve", lambda e, pa=pa, st=st: e.tensor_copy(out=st, in_=pa[:, 0:512]), reads=[pt], writes=[t_st])
                    sq, t_sq, _ = sq_r.next()
                    P.op("act", lambda e, pa=pa, sq=sq: e.activation(out=sq, in_=pa[:, 0:512], func=AF.Square), reads=[pt], writes=[t_sq])
                    sumsq_accum(sq, t_sq, accs[half], m == 0, m == FT - 1, 512)
                    P.op("sp", lambda e, st=st, m=m, half=half: e.dma_start(out=FFN_scr[m, :, half * 512:(half + 1) * 512], in_=st), reads=[t_st], writes=[T_FFN[m][half]], dma_sem=s_st)
            rstd_from_psum(accs, D, rstd_f, t_rstd_f, CH2)
            P.barrier(dummy[:, 0:1], skip=wtok)
            ckpt(10)

            A.reset(A_BASE1)
            rstd_keep = A.alloc([NT], F32)
            P.op("dve", lambda e: e.tensor_copy(out=rstd_keep, in_=rstd_f), reads=[t_rstd_f], writes=[t_rk])
            P.barrier(dummy[:, 0:1], skip=wtok)
            ckpt(11)
            h3 = A.alloc([FT, NT], BF16)
            t_h3 = P.toks(FT)
            H_END = A.off
            resid_stage(FFN_scr, T_FFN, rstd_keep, t_rk, C_GFFNPOST, "scr", X1_scr, T_X1, X2_scr, T_X2, C_GPLEPRE, h3, t_h3)
            P.barrier(dummy[:, 0:1], skip=wtok)
            ckpt(12)

            A.reset(H_END)
            pT = A.alloc([2, NT], BF16)
            t_pT = P.toks(2)
            pring_ = SRing(4, [256], BF16)
            for gg in range(2):
                front_end(lambda i, gg=gg: p_own[(gg * 4 + i) * 128:(gg * 4 + i + 1) * 128, :], 4, pring_, None, None, None, None, None, pT, t_pT, gg * 512, KT=2, width=256)
            rh3 = lambda kt: (h3[:, kt, :], t_h3[kt])
            rpT = lambda kt: (pT[:, kt, :], t_pT[kt])
            sgr = SRing(2, [512], F32, with_sem=False)

            def ple_post_factory():
                state = {}

                def post(m, ci, pa, pt, dst, t_dst):
                    if ci == 0:
                        state["pp"] = proj(w_pp, m, 2, [(rpT, 0, 512), (rpT, 512, 512)], nrr=6)
                    sg, t_sg, _ = sgr.next()
                    P.op("act", lambda e, sg=sg, pa=pa: e.activation(out=sg, in_=pa[:, 0:512], func=AF.Sigmoid), reads=[pt], writes=[t_sg])
                    pp, ppt = state["pp"][ci]
                    P.op("dve", lambda e, sg=sg, pp=pp, dst=dst: e.tensor_tensor(out=dst, in0=pp[:, 0:512], in1=sg, op=OP.mult), reads=[ppt, t_sg], writes=[t_dst])
                return post

            PLE_BASE = A.off
            rstd_p, t_rstd_p = out_proj_stage(w_pg, FT, rh3, PRD_scr, T_PRD, PLE_BASE, post=ple_post_factory())
            P.barrier(dummy[:, 0:1], skip=wtok)
            ckpt(13)
            A.reset(A_BASE1)
            rstd_keep = A.alloc([NT], F32)
            P.op("dve", lambda e: e.tensor_copy(out=rstd_keep, in_=rstd_p), reads=[t_rstd_p], writes=[t_rk])
            P.barrier(dummy[:, 0:1], skip=wtok)
            ckpt(14)
            resid_stage(PRD_scr, T_PRD, rstd_keep, t_rk, C_GPLEPOST, "scr", X2_scr, T_X2, None, None, None, None, None, final=True)
        try:
            body()
        except _Stop:
            pass
        P.op("sp", lambda e: None, reads=T_OUT)
        P.finalize(nc, es)
    return nc


def _tile_w(W, KT, MT):
    return np.ascontiguousarray(W.reshape(KT, 128, MT, 128).transpose(2, 1, 0, 3))


_NC_CACHE = {}


def _prep(x, p, positions, norm_mix_pre, norm_mix_post, w_in, q_norm, kv_norm, w_q_b, w_kv_b,
           w_pool, pool_scale, w_up_pool, w_up_mla, w_branch_gate, w_out, norm_ffn_pre,
           norm_ffn_post, w_ffn_gate, w_ffn_up, w_ffn_down, norm_ple_pre, w_ple_gate,
           w_ple_proj, norm_ple_post):
    f32 = np.float32
    x = np.asarray(x, f32)[0]
    p = np.asarray(p, f32)[0, 0]
    pos = np.asarray(positions, np.int32)[0]
    w_in = np.asarray(w_in, f32)[0]

    shared = {}
    shared["x_all"] = x
    shared["pos_all"] = np.ascontiguousarray(np.broadcast_to(pos[None, :], (64, S)))
    shared["w_in_main"] = _tile_w(w_in[:, 0:3072], 32, 24)
    shared["w_in_kv"] = np.ascontiguousarray(w_in[:, 3072:3648].reshape(32, 128, 576).transpose(1, 0, 2))
    wkv = np.asarray(w_kv_b, f32)[0]
    shared["w_k"] = np.ascontiguousarray(wkv[:, :, 0:128].reshape(4, 128, NH * 128).transpose(1, 0, 2))
    shared["w_v"] = np.ascontiguousarray(wkv[:, :, 128:256].reshape(4, 128, NH * 128).transpose(1, 0, 2))
    wq = np.asarray(w_q_b, f32)[0]
    shared["w_q"] = np.ascontiguousarray(wq.reshape(8, 128, NH, 192).transpose(2, 1, 0, 3))
    wp = np.asarray(w_pool, f32)[0]
    shared["w_pool"] = np.ascontiguousarray(wp.reshape(4, 4, 128, 4, 128).transpose(0, 3, 2, 1, 4).reshape(16, 128, 4, 128))
    shared["w_up_pool"] = _tile_w(np.asarray(w_up_pool, f32)[0], 16, 32)
    shared["w_up_mla"] = _tile_w(np.asarray(w_up_mla, f32)[0], 16, 32)
    shared["w_gate"] = _tile_w(np.asarray(w_branch_gate, f32)[0].reshape(D, 2 * D), 32, 64)
    shared["w_out"] = _tile_w(np.asarray(w_out, f32)[0], 32, 32)
    shared["w_fg"] = _tile_w(np.asarray(w_ffn_gate, f32)[0], 32, FFT)
    shared["w_fu"] = _tile_w(np.asarray(w_ffn_up, f32)[0], 32, FFT)
    shared["w_fd"] = _tile_w(np.asarray(w_ffn_down, f32)[0], FFT, 32)
    shared["w_pg"] = _tile_w(np.asarray(w_ple_gate, f32)[0], 32, 32)
    shared["w_pp"] = _tile_w(np.asarray(w_ple_proj, f32)[0], 2, 32)
    mats = np.zeros((128, 3, 128), f32)
    mats[:, 0, :] = np.eye(128, dtype=f32)
    mats[:, 1, :] = 1.0
    for j in range(32):
        mats[32 + j, 2, j] = -1.0
        mats[j, 2, 32 + j] = 1.0
    shared["mats"] = mats

    def colmajor(v, nt):
        return np.asarray(v, f32).reshape(nt, 128).T

    cst0 = np.zeros((128, C_NCOL), f32)
    cst0[:, C_GMIXPRE:C_GMIXPRE + 32] = colmajor(norm_mix_pre[0], 32)
    cst0[:, C_GMIXPOST:C_GMIXPOST + 32] = colmajor(norm_mix_post[0], 32)
    cst0[:, C_GFFNPRE:C_GFFNPRE + 32] = colmajor(norm_ffn_pre[0], 32)
    cst0[:, C_GFFNPOST:C_GFFNPOST + 32] = colmajor(norm_ffn_post[0], 32)
    cst0[:, C_GPLEPRE:C_GPLEPRE + 32] = colmajor(norm_ple_pre[0], 32)
    cst0[:, C_GPLEPOST:C_GPLEPOST + 32] = colmajor(norm_ple_post[0], 32)
    cst0[:, C_QN:C_QN + 8] = colmajor(q_norm[0], 8)
    cst0[:, C_KVN:C_KVN + 4] = colmajor(kv_norm[0], 4)
    cst0[:, C_PSC:C_PSC + 16] = colmajor(pool_scale[0], 16)
    invf = (10000.0 ** (-np.arange(0, 64, 2, dtype=f32) / 64)).astype(f32)
    cst0[0:32, C_INVF] = invf
    cst0[32:64, C_INVF] = invf
    cst0[:, C_EPS] = EPS

    xb = x.reshape(8, 8, 128, D)
    pb = p.reshape(8, 8, 128, 256)
    posb = pos.reshape(8, 8, 128)
    in_maps = []
    for c in range(NCORE):
        m = dict(shared)
        m["x_own"] = np.ascontiguousarray(xb[:, c]).reshape(NT, D)
        m["p_own"] = np.ascontiguousarray(pb[:, c]).reshape(NT, 256)
        m["pos_own"] = np.ascontiguousarray(np.broadcast_to(posb[:, c].reshape(1, NT), (64, NT)))
        halo = np.zeros((8, 16, D), f32)
        for i in range(8):
            t0 = (8 * i + c) * 128
            if t0 >= 16:
                halo[i] = x[t0 - 16:t0]
        m["x_halo"] = halo.reshape(128, D)
        cst = cst0.copy()
        for cp in range(8):
            for half in range(2):
                col = C_MASK + cp * 2 + half
                if cp > c:
                    cst[:, col] = NEG
                elif cp == c and half == 0:
                    cst[64:128, col] = NEG
        if c == 0:
            for g in range(4):
                w = 2 ** (g + 1)
                for t in range(16):
                    cst[:, C_PCORR + g * 16 + t] = 1.0 / min(w, t + 1) - 1.0 / w
        m["cst"] = cst
        mk = np.zeros((128, 8, 128), f32)
        for cp in range(8):
            if cp > c:
                mk[:, cp, :] = NEG
            elif cp == c:
                mk[64:128, cp, 0:64] = NEG
        m["maskm"] = mk
        in_maps.append(m)

    return in_maps


def kernel(x, p, positions, norm_mix_pre, norm_mix_post, w_in, q_norm, kv_norm, w_q_b, w_kv_b,
           w_pool, pool_scale, w_up_pool, w_up_mla, w_branch_gate, w_out, norm_ffn_pre,
           norm_ffn_post, w_ffn_gate, w_ffn_up, w_ffn_down, norm_ple_pre, w_ple_gate,
           w_ple_proj, norm_ple_post):
    in_maps = _prep(x, p, positions, norm_mix_pre, norm_mix_post, w_in, q_norm, kv_norm, w_q_b, w_kv_b,
                    w_pool, pool_scale, w_up_pool, w_up_mla, w_branch_gate, w_out, norm_ffn_pre,
                    norm_ffn_post, w_ffn_gate, w_ffn_up, w_ffn_down, norm_ple_pre, w_ple_gate,
                    w_ple_proj, norm_ple_post)
    f32 = np.float32
    if "nc" not in _NC_CACHE:
        _NC_CACHE["nc"] = build_program()
    nc = _NC_CACHE["nc"]
    res = run_bass_kernel_spmd(nc, in_maps, core_ids=list(range(NCORE)))
    out = np.empty((8, 8, 128, D), f32)
    for c in range(NCORE):
        out[:, c] = res.results[c]["out"].reshape(8, 128, D)
    return out.reshape(1, S, D)
```
